# Optimizing a Trainium2 kernel written in Bass

```python
import math
import jax, jax.numpy as jnp
from jax import lax
import numpy as np

D_MODEL = 1024
BATCH = 8
SEQ = 4096
DEPTH = 1

GDN_HEADS = 8
GDN_DK = 128
GDN_DV = 128
GDN_CONV = 4
GDN_CHUNK = 64
DSA_HEADS = 8
DSA_KV_HEADS = 2
DSA_DH = 128
IDX_HEADS = 8
IDX_DIM = 64
TOPK_MAX = 256
Q_BLOCK = 128
D_FF = 2816
FFN_CONV = 3
EPS = 1e-6

GDN_QK_W = GDN_HEADS * GDN_DK
GDN_V_W = GDN_HEADS * GDN_DV
GDN_CONV_CH = 2 * GDN_QK_W + GDN_V_W
DSA_Q_W = DSA_HEADS * DSA_DH
DSA_KV_W = DSA_KV_HEADS * DSA_DH
IDX_Q_W = IDX_HEADS * IDX_DIM
IN_SPLITS = (GDN_CONV_CH, GDN_V_W, GDN_HEADS, GDN_HEADS,
             DSA_Q_W, DSA_KV_W, DSA_KV_W, IDX_Q_W, IDX_DIM, IDX_HEADS,
             D_MODEL, D_MODEL)
N_IN = GDN_CONV_CH + GDN_V_W + 2 * GDN_HEADS + DSA_Q_W + 2 * DSA_KV_W + IDX_Q_W + IDX_DIM + IDX_HEADS + 2 * D_MODEL

kernel_name = "hybrid_gdn_dsa_convglu_block"


def rmsnorm(x, g):
    xf = x.astype(jnp.float32)
    y = xf * lax.rsqrt(jnp.mean(xf * xf, axis=-1, keepdims=True) + EPS)
    return (y * g.astype(jnp.float32)).astype(x.dtype)


def layernorm_nobias(x, g):
    xf = x.astype(jnp.float32)
    xc = xf - jnp.mean(xf, axis=-1, keepdims=True)
    y = xc * lax.rsqrt(jnp.mean(xc * xc, axis=-1, keepdims=True) + EPS)
    return (y * g.astype(jnp.float32)).astype(x.dtype)


def l2norm(x):
    xf = x.astype(jnp.float32)
    return xf * lax.rsqrt(jnp.sum(xf * xf, axis=-1, keepdims=True) + EPS)


def causal_dwconv(x, w):
    width, ch = w.shape
    return lax.conv_general_dilated(
        x, w[:, None, :].astype(x.dtype), window_strides=(1,),
        padding=[(width - 1, 0)], dimension_numbers=("NWC", "WIO", "NWC"),
        feature_group_count=ch)


def gated_delta_rule_chunked(q, k, v, g, beta):
    B, S, H, dk = q.shape
    dv = v.shape[-1]
    C = GDN_CHUNK
    N = S // C
    q = q * (dk ** -0.5)

    def chunks(t):
        return jnp.swapaxes(t.reshape((B, N, C) + t.shape[2:]), 2, 3)

    qc, kc, vc, gc, bc = chunks(q), chunks(k), chunks(v), chunks(g), chunks(beta)
    G = jnp.cumsum(gc, axis=-1)
    tri = jnp.tril(jnp.ones((C, C), dtype=bool))
    strict = jnp.tril(jnp.ones((C, C), dtype=bool), -1)
    gamma = jnp.exp(jnp.where(tri, G[..., :, None] - G[..., None, :], -jnp.inf))
    kb = kc * bc[..., None]
    A = jnp.where(strict, jnp.einsum("bnhid,bnhjd->bnhij", kb, kc) * gamma, 0.0)
    eye = jnp.eye(C, dtype=jnp.float32)
    T = lax.linalg.triangular_solve(A + eye, jnp.broadcast_to(eye, A.shape),
                                    left_side=True, lower=True)
    eG = jnp.exp(G)[..., None]
    u = jnp.einsum("bnhij,bnhje->bnhie", T, vc * bc[..., None])
    w = jnp.einsum("bnhij,bnhjd->bnhid", T, kb * eG)
    qg = qc * eG
    kd = kc * jnp.exp(G[..., -1:] - G)[..., None]
    att = jnp.einsum("bnhid,bnhjd->bnhij", qc, kc) * gamma
    dlast = jnp.exp(G[..., -1])

    def step(state, inp):
        u_i, w_i, qg_i, kd_i, att_i, dl_i = inp
        v_new = u_i - jnp.einsum("bhcd,bhde->bhce", w_i, state)
        o = jnp.einsum("bhcd,bhde->bhce", qg_i, state) + jnp.einsum("bhij,bhje->bhie", att_i, v_new)
        state = state * dl_i[..., None, None] + jnp.einsum("bhcd,bhce->bhde", kd_i, v_new)
        return state, o

    xs = (jnp.swapaxes(u, 0, 1), jnp.swapaxes(w, 0, 1), jnp.swapaxes(qg, 0, 1),
          jnp.swapaxes(kd, 0, 1), jnp.swapaxes(att, 0, 1), jnp.swapaxes(dlast, 0, 1))
    s0 = jnp.zeros((B, H, dk, dv), jnp.float32)
    _, o = lax.scan(step, s0, xs)
    return jnp.transpose(o, (1, 0, 3, 2, 4)).reshape(B, S, H, dv)


def dsa_sparse_attention(q, k, v, iq, ik, iw):
    B, S, Hq, dh = q.shape
    L, Hkv = k.shape[1], k.shape[2]
    k_top = min(TOPK_MAX, L // 4)
    rep = Hq // Hkv
    nb = S // Q_BLOCK
    key_pos = jnp.arange(L, dtype=jnp.int32)
    starts = jnp.arange(nb, dtype=jnp.int32) * Q_BLOCK
    iw = iw * ((IDX_HEADS ** -0.5) * (IDX_DIM ** -0.5))

    def blocks(t):
        return jnp.swapaxes(t.reshape((B, nb, Q_BLOCK) + t.shape[2:]), 0, 1)

    def one_block(args):
        q_b, iq_b, iw_b, start = args
        t_pos = start + jnp.arange(Q_BLOCK, dtype=jnp.int32)
        causal = key_pos[None, :] <= t_pos[:, None]
        rel = jax.nn.relu(jnp.einsum("bqhd,bsd->bqhs", iq_b, ik))
        score = jnp.einsum("bqh,bqhs->bqs", iw_b, rel).astype(jnp.float32)
        score = jnp.where(causal[None], score, -jnp.inf)
        _, idx = lax.top_k(score, k_top)
        valid = idx <= t_pos[None, :, None]
        k_sel = jax.vmap(lambda kb, ib: kb[ib])(k, idx)
        v_sel = jax.vmap(lambda vb, ib: vb[ib])(v, idx)
        qg = q_b.reshape(B, Q_BLOCK, Hkv, rep, dh)
        logits = jnp.einsum("bqgrd,bqkgd->bqgrk", qg, k_sel).astype(jnp.float32) * (dh ** -0.5)
        logits = jnp.where(valid[:, :, None, None, :], logits, -jnp.inf)
        p = jax.nn.softmax(logits, axis=-1).astype(v.dtype)
        o = jnp.einsum("bqgrk,bqkgd->bqgrd", p, v_sel)
        return o.reshape(B, Q_BLOCK, Hq * dh)

    out = lax.map(one_block, (blocks(q), blocks(iq), blocks(iw), starts))
    return jnp.swapaxes(out, 0, 1).reshape(B, S, Hq * dh)


def setup_inputs(seed: int = 0) -> dict:
    key = jax.random.key(seed)
    ks = jax.random.split(key, 16)
    f32 = jnp.float32

    def nrm(k, shape, scale):
        return jax.random.normal(k, shape, f32) * scale

    x = nrm(ks[0], (BATCH, SEQ, D_MODEL), 1.0)
    norm1_g = 1.0 + nrm(ks[1], (DEPTH, D_MODEL), 0.02)
    w_in = nrm(ks[2], (DEPTH, D_MODEL, N_IN), D_MODEL ** -0.5)
    gdn_conv_w = nrm(ks[3], (DEPTH, GDN_CONV, GDN_CONV_CH), GDN_CONV ** -0.5)
    gdn_a_log = jnp.log(jax.random.uniform(ks[4], (DEPTH, GDN_HEADS), f32, 1.0, 16.0))
    dt = jnp.exp(jax.random.uniform(ks[5], (DEPTH, GDN_HEADS), f32, math.log(1e-3), math.log(1e-1)))
    gdn_dt_bias = dt + jnp.log(-jnp.expm1(-dt))
    gdn_norm_g = 1.0 + nrm(ks[6], (DEPTH, GDN_DV), 0.02)
    idx_k_norm_g = 1.0 + nrm(ks[7], (DEPTH, IDX_DIM), 0.02)
    branch_gate_b = nrm(ks[8], (DEPTH, 2 * D_MODEL), 0.02)
    w_out = nrm(ks[9], (DEPTH, D_MODEL, D_MODEL), D_MODEL ** -0.5)
    norm2_g = 1.0 + nrm(ks[10], (DEPTH, D_MODEL), 0.02)
    ffn_w_up = nrm(ks[11], (DEPTH, D_MODEL, 2 * D_FF), D_MODEL ** -0.5)
    ffn_conv_w = nrm(ks[12], (DEPTH, FFN_CONV, 2 * D_FF), FFN_CONV ** -0.5)
    ffn_conv_b = nrm(ks[13], (DEPTH, 2 * D_FF), 0.02)
    ffn_w_down = nrm(ks[14], (DEPTH, D_FF, D_MODEL), D_FF ** -0.5)
    final_g = 1.0 + nrm(ks[15], (D_MODEL,), 0.02)
    return {"x": x, "norm1_g": norm1_g, "w_in": w_in, "gdn_conv_w": gdn_conv_w,
            "gdn_a_log": gdn_a_log, "gdn_dt_bias": gdn_dt_bias, "gdn_norm_g": gdn_norm_g,
            "idx_k_norm_g": idx_k_norm_g, "branch_gate_b": branch_gate_b, "w_out": w_out,
            "norm2_g": norm2_g, "ffn_w_up": ffn_w_up, "ffn_conv_w": ffn_conv_w,
            "ffn_conv_b": ffn_conv_b, "ffn_w_down": ffn_w_down, "final_g": final_g}


def reference(x, norm1_g, w_in, gdn_conv_w, gdn_a_log, gdn_dt_bias, gdn_norm_g,
              idx_k_norm_g, branch_gate_b, w_out, norm2_g, ffn_w_up, ffn_conv_w,
              ffn_conv_b, ffn_w_down, final_g):
    B, S, _ = x.shape
    offsets = [int(o) for o in np.cumsum(IN_SPLITS)[:-1]]
    h = x
    for l in range(DEPTH):
        u = rmsnorm(h, norm1_g[l])
        proj = jnp.einsum("bsd,de->bse", u, w_in[l])
        (g_qkv, g_z, g_a, g_b, d_q, d_k, d_v, i_q, i_k, i_w, gate_a, gate_b) = jnp.split(proj, offsets, axis=-1)

        qkv = jax.nn.silu(causal_dwconv(g_qkv, gdn_conv_w[l]))
        gq, gk, gv = jnp.split(qkv, [GDN_QK_W, 2 * GDN_QK_W], axis=-1)
        gq = l2norm(gq.reshape(B, S, GDN_HEADS, GDN_DK))
        gk = l2norm(gk.reshape(B, S, GDN_HEADS, GDN_DK))
        gv = gv.reshape(B, S, GDN_HEADS, GDN_DV).astype(jnp.float32)
        beta = jax.nn.sigmoid(g_b.astype(jnp.float32))
        logdecay = -jnp.exp(gdn_a_log[l].astype(jnp.float32)) * jax.nn.softplus(
            g_a.astype(jnp.float32) + gdn_dt_bias[l].astype(jnp.float32))
        o_gdn = gated_delta_rule_chunked(gq, gk, gv, logdecay, beta)
        z = g_z.reshape(B, S, GDN_HEADS, GDN_DV).astype(jnp.float32)
        o_gdn = (rmsnorm(o_gdn, gdn_norm_g[l]) * jax.nn.silu(z)).reshape(B, S, GDN_V_W).astype(x.dtype)

        o_dsa = dsa_sparse_attention(
            d_q.reshape(B, S, DSA_HEADS, DSA_DH),
            d_k.reshape(B, S, DSA_KV_HEADS, DSA_DH),
            d_v.reshape(B, S, DSA_KV_HEADS, DSA_DH),
            i_q.reshape(B, S, IDX_HEADS, IDX_DIM),
            layernorm_nobias(i_k, idx_k_norm_g[l]),
            i_w)

        gb = branch_gate_b[l]
        mixed = (jax.nn.sigmoid(gate_a + gb[:D_MODEL]) * o_gdn
                 + jax.nn.sigmoid(gate_b + gb[D_MODEL:]) * o_dsa)
        h = h + jnp.einsum("bse,ed->bsd", mixed, w_out[l])

        hn = rmsnorm(h, norm2_g[l])
        up = causal_dwconv(jnp.einsum("bsd,df->bsf", hn, ffn_w_up[l]), ffn_conv_w[l]) + ffn_conv_b[l]
        f_gate, f_val = jnp.split(up, [D_FF], axis=-1)
        h = h + jnp.einsum("bsf,fd->bsd", jax.nn.silu(f_gate) * f_val, ffn_w_down[l])
    return rmsnorm(h, final_g)
```

```python
import numpy as np
from contextlib import ExitStack
import concourse.bass as bass
import concourse.mybir as mybir
from concourse.bass_utils import run_bass_kernel_spmd

F32 = mybir.dt.float32
BF16 = mybir.dt.bfloat16
ALU = mybir.AluOpType
AF = mybir.ActivationFunctionType
AX = mybir.AxisListType

D = 1024
NIN = 8280
DFF = 2816
EPS = 1e-6
TOPK = 256
O_QKV, O_Z, O_A, O_B, O_DQ, O_DK, O_DV, O_IQ, O_IK, O_IW, O_GA, O_GB = (
    0, 3072, 4096, 4104, 4112, 5136, 5392, 5648, 6160, 6224, 6232, 7256)


class Reg:
    __slots__ = ("name", "w", "r")

    def __init__(self, name=""):
        self.name = name
        self.w = None
        self.r = []


def regs(n, name=""):
    return [Reg("%s%d" % (name, i)) for i in range(n)]


class Sched:
    NDMA = 8

    def __init__(self, nc):
        self.nc = nc
        self.eng = {"pe": nc.tensor, "act": nc.scalar, "dve": nc.vector,
                    "pool": nc.gpsimd, "sp": nc.sync}
        self.sems = []
        self.esem = {}
        for k in self.eng:
            self.esem[k] = self._newsem("e_" + k)
        self.cnt = {k: 0 for k in self.eng}
        self.waited = {k: {} for k in self.eng}
        self.latest = {}
        self.dsem, self.duse, self.dnext = {}, {}, {}
        self.ninst = 0
        self.nwaits = 0

    def _newsem(self, name):
        self.sems.append(self.nc.alloc_semaphore(name=name))
        return len(self.sems) - 1

    def _deps(self, reads, writes, excl=(), own=None, fast=False):
        deps = {}

        def add(s, v):
            if deps.get(s, 0) < v:
                deps[s] = v
        for r in reads:
            if r.w is not None:
                add(*r.w)
        for r in excl:
            if r.w is not None:
                add(*r.w)
            for (s, v) in r.r:
                if s != own:
                    add(s, v)
        for w in writes:
            if w.w is not None:
                add(*w.w)
            for (s, v) in w.r:
                if fast and s == own:
                    continue
                add(s, v)
        return deps

    def _emit_waits(self, en, deps):
        e = self.eng[en]
        wd = self.waited[en]
        own = self.esem[en]
        for s, v in deps.items():
            if s == own and en == "pe":
                continue
            if wd.get(s, 0) >= v:
                continue
            e.wait_ge(self.sems[s], v)
            wd[s] = v
            self.nwaits += 1

    def _commit(self, ev, reads, writes):
        self.latest[ev[0]] = ev[1]
        for r in reads:
            r.r = [(s, v) for (s, v) in r.r if s != ev[0]] + [ev]
        for w in writes:
            w.w = ev
            w.r = []

    def op(self, en, fn, reads=(), writes=(), excl=()):
        own = self.esem[en]
        self._emit_waits(en, self._deps(reads, writes, excl, own, False))
        ins = fn(self.eng[en])
        ins.then_inc(self.sems[own], 1)
        self.cnt[en] += 1
        self.ninst += 1
        self._commit((own, self.cnt[en]), list(reads) + list(excl), writes)

    def dma(self, q, out, in_, reads=(), writes=(), **kw):
        if q not in self.dsem:
            self.dsem[q] = [self._newsem("d_%s%d" % (q, i)) for i in range(self.NDMA)]
            self.duse[q] = [0] * self.NDMA
            self.dnext[q] = 0
        i = self.dnext[q]
        self.dnext[q] = (i + 1) % self.NDMA
        s = self.dsem[q][i]
        deps = self._deps(reads, writes)
        if self.duse[q][i] > 0:
            v = 16 * self.duse[q][i]
            if deps.get(s, 0) < v:
                deps[s] = v
        self._emit_waits(q, deps)
        ins = self.eng[q].dma_start(out=out, in_=in_, **kw)
        ins.then_inc(self.sems[s], 16)
        self.duse[q][i] += 1
        self.ninst += 1
        self._commit((s, 16 * self.duse[q][i]), reads, writes)

    def barrier(self):
        for en in self.eng:
            self._emit_waits(en, dict(self.latest))


class Ctx:
    pass


def build(S=4096, dbg=()):
    NT = S // 128
    NG = S // 512
    nc = bass.Bass("TRN2", target_bir_lowering=False)
    c = Ctx()
    c.nc, c.S, c.NT, c.NG = nc, S, NT, NG
    for d_ in dbg:
        if d_.startswith("cstop="):
            c.cstop = int(d_[6:])
    sc = c.sc = Sched(nc)

    def din(name, shape):
        return nc.dram_tensor(name, shape, F32, kind="ExternalInput").ap()

    c.x = din("x", [S, D])
    c.norm1_g = din("norm1_g", [1, D])
    c.w_in = din("w_in", [D, NIN])
    c.gdn_conv_w = din("gdn_conv_w", [128, 24, 4])
    c.gdn_a_log = din("gdn_a_log", [1, 8])
    c.gdn_dt_bias = din("gdn_dt_bias", [1, 8])
    c.gdn_norm_g = din("gdn_norm_g", [128, 1])
    c.idx_k_norm_g = din("idx_k_norm_g", [1, 64])
    c.branch_gate_b = din("branch_gate_b", [128, 16])
    c.w_out = din("w_out", [D, D])
    c.norm2_g = din("norm2_g", [1, D])
    c.ffn_w_up = din("ffn_w_up", [D, 2 * DFF])
    c.ffn_conv_w = din("ffn_conv_w", [128, 44, 3])
    c.ffn_conv_b = din("ffn_conv_b", [128, 44])
    c.ffn_w_down = din("ffn_w_down", [DFF, D])
    c.final_g = din("final_g", [1, D])
    c.out = nc.dram_tensor("out", [S, D], F32, kind="ExternalOutput").ap()
    c.r_out = Reg("out")

    def scratch(name, shape, dt):
        kind = "ExternalOutput" if name in dbg else "Internal"
        if ("in_" + name) in dbg:
            kind = "ExternalInput"
        return nc.dram_tensor(name, shape, dt, kind=kind).ap()

    c.s_gq = scratch("s_gq", [24, 128, S], BF16)
    c.s_fa = scratch("s_fa", [8, 128, S], BF16)
    c.s_dq = scratch("s_dq", [8, 128, S], BF16)
    c.s_dk = scratch("s_dk", [2, 128, S], BF16)
    c.s_iq = scratch("s_iq", [4, 128, S], BF16)
    c.s_gb = scratch("s_gb", [8, 128, S], BF16)
    c.s_ik = scratch("s_ik", [128, S], BF16)
    c.s_v = scratch("s_v", [S, 256], BF16)
    c.s_tok = scratch("s_tok", [S, 32], F32)
    c.s_mg = scratch("s_mg", [8, 128, S], BF16)
    c.s_h = scratch("s_h", [S, D], F32)
    c.r_gq = regs(24, "gq")
    c.r_fa = regs(8, "fa")
    c.r_dq = regs(8, "dq")
    c.r_dk = regs(2, "dk")
    c.r_iq = regs(4, "iq")
    c.r_gb = regs(8, "gb")
    c.r_ik = Reg("ik")
    c.r_v = Reg("v")
    c.r_tok = Reg("tok")
    c.r_mg = regs(NT, "mg")
    c.r_h = regs(NT, "h")

    with ExitStack() as top:
        def sb(name, shape, dt, es=top):
            return es.enter_context(nc.sbuf_tensor(name, shape, dt))
        c.sb = sb
        c.identf = sb("identf", [128, 128], F32)
        c.ident = sb("ident", [128, 128], BF16)
        c.ones_bf = sb("ones_bf", [128, 128], BF16)
        c.r_const = Reg("const")
        sc.op("pool", lambda e: e.memset(c.identf[:], 0.0), writes=[c.r_const])
        sc.op("pool", lambda e: e.affine_select(
            out=c.identf[:], in_=c.identf[:], pattern=[[-1, 128]], compare_op=ALU.not_equal,
            fill=1.0, base=0, channel_multiplier=1), reads=[c.r_const], writes=[c.r_const])
        sc.op("dve", lambda e: e.tensor_copy(out=c.ident[:], in_=c.identf[:]), reads=[c.r_const], writes=[c.r_const])
        sc.op("dve", lambda e: e.memset(c.ones_bf[:], 1.0), reads=[c.r_const], writes=[c.r_const])

        if "only_f" not in dbg:
            with ExitStack() as esab:
                c.actT = sb("uT", [128, 8, S], BF16, esab)
                c.r_actT = regs(NT, "uT")
                phase_a(c, c.x, c.norm1_g, None)
                phase_b(c)
                sc.barrier()
            if "stop_b" in dbg:
                return nc
            phase_c(c)
            if "stop_c" in dbg:
                return nc
            phase_d(c)
            if "stop_d" in dbg:
                return nc
        phase_f(c)
        sc.barrier()
    return nc


def norm_transpose(c, es, tiles, g_ap, name):
    nc, sc = c.nc, c.sc
    sb = lambda n, s, d: c.sb(name + n, s, d, es)
    gb = sb("gb", [128, D], F32)
    xt = [sb("xt%d" % i, [128, D], F32) for i in range(2)]
    junk = sb("junk", [128, D], BF16)
    ub = [sb("ub%d" % i, [128, D], BF16) for i in range(2)]
    ss = [sb("ss%d" % i, [128, 1], F32) for i in range(2)]
    rstd = [sb("rstd%d" % i, [128, 1], F32) for i in range(2)]
    pT = [es.enter_context(nc.psum_tensor(name + "pT%d" % i, [128, 8, 128], BF16)) for i in range(2)]
    r_gb, r_junk = Reg(), Reg()
    r_xt, r_ub, r_ss, r_rstd, r_pT = regs(2), regs(2), regs(2), regs(2), regs(2)
    sc.dma("sp", gb[:], g_ap.partition_broadcast(128), writes=[r_gb])
    tiles = list(tiles)

    def s1(i):
        t, loader = tiles[i]
        b = i % 2
        loader(xt[b], r_xt[b])
        norm_transpose_tile(c, t, xt[b], r_xt[b], gb, r_gb, junk, r_junk, ub[b], r_ub[b], ss[b], r_ss[b],
                            rstd[b], r_rstd[b], pT[b], r_pT[b], part=1)

    def s2(i):
        t, loader = tiles[i]
        b = i % 2
        norm_transpose_tile(c, t, xt[b], r_xt[b], gb, r_gb, junk, r_junk, ub[b], r_ub[b], ss[b], r_ss[b],
                            rstd[b], r_rstd[b], pT[b], r_pT[b], part=2)
    if tiles:
        s1(0)
    for i in range(len(tiles)):
        if i + 1 < len(tiles):
            s1(i + 1)
        s2(i)


def norm_transpose_tile(c, t, xt, r_xt, gb, r_gb, junk, r_junk, ub, r_ub, ss, r_ss, rstd, r_rstd, pT, r_pT,
                        dst=None, r_dst=None, part=0):
    sc = c.sc
    if dst is None:
        dst, r_dst = c.actT[:, :, t * 128:(t + 1) * 128], c.r_actT[t]
    if part in (0, 1):
        sc.op("act", lambda e: e.activation(out=junk[:], in_=xt[:], func=AF.Square, accum_out=ss[:]),
              reads=[r_xt], writes=[r_junk, r_ss])
        sc.op("act", lambda e: e.activation(out=rstd[:], in_=ss[:], func=AF.Ln, bias=EPS, scale=1.0 / D),
              reads=[r_ss], writes=[r_rstd])
        sc.op("act", lambda e: e.activation(out=rstd[:], in_=rstd[:], func=AF.Exp, scale=-0.5),
              reads=[r_rstd], writes=[r_rstd])
    if part == 1:
        return
    sc.op("dve", lambda e: e.scalar_tensor_tensor(out=ub[:], in0=xt[:], scalar=rstd[:, 0:1], in1=gb[:],
                                                  op0=ALU.mult, op1=ALU.mult),
          reads=[r_xt, r_rstd, r_gb], writes=[r_ub])

    def tr(e):
        for kc in range(8):
            ins = e.transpose(out=pT[:, kc, :], in_=ub[:, kc * 128:(kc + 1) * 128], identity=c.ident[:])
        return ins
    sc.op("pe", tr, reads=[r_ub, c.r_const], writes=[r_pT])
    sc.op("act", lambda e: e.copy(out=dst, in_=pT[:]), reads=[], writes=[r_dst], excl=[r_pT])


def phase_a(c, x, g, _):
    sc = c.sc
    with ExitStack() as es:
        def mk(t):
            def loader(dst, reg):
                sc.dma("sp", dst[:], x[t * 128:(t + 1) * 128, :], writes=[reg])
            return loader
        norm_transpose(c, es, [(t, mk(t)) for t in range(c.NT)], g, "pa_")
        sc.barrier()


def phase_b(c):
    nc, sc, S, NT, NG = c.nc, c.sc, c.S, c.NT, c.NG
    wv = c.w_in.rearrange("(kc p) n -> p kc n", p=128)
    with ExitStack() as es:
        sb = lambda n, s, d: c.sb("pb_" + n, s, d, es)
        ps = lambda n, s, d=F32: es.enter_context(nc.psum_tensor("pb_" + n, s, d))
        NW = 8
        wbf = [sb("w%d" % i, [128, 8, 128], BF16) for i in range(NW)]
        r_w = regs(NW)
        pm = [ps("pm%d" % i, [128, 512]) for i in range(4)]
        r_pm = regs(4)
        acc = [sb("acc%d" % i, [128, S], F32) for i in range(2)]
        r_acc = regs(2)
        yb = [sb("yb%d" % i, [128, S], BF16) for i in range(2)]
        r_yb = regs(2)
        sq = sb("sq", [128, S], BF16)
        r_sq = Reg()
        rn = sb("rn", [128, S], F32)
        r_rn = Reg()
        cw = sb("cw", [128, 24, 4], F32)
        gbias = sb("gbias", [128, 16], F32)
        gng = sb("gng", [128, 1], F32)
        r_cst = Reg()
        sc.dma("sp", cw[:], c.gdn_conv_w, writes=[r_cst])
        sc.dma("sp", gbias[:], c.branch_gate_b, writes=[r_cst])
        sc.dma("sp", gng[:], c.gdn_norm_g, writes=[r_cst])

        state = {"w": 0, "pm": 0, "row": 0}

        def row_block(col0, ncols, dst, r_dst, part0=0, foff=0):
            wi = state["w"] % NW
            state["w"] += 1
            sc.dma("pool", wbf[wi][:, :, 0:ncols], wv[:, :, col0:col0 + ncols], writes=[r_w[wi]])
            for g in range(NG):
                pi = state["pm"] % 4
                state["pm"] += 1

                def mm(e):
                    for kc in range(8):
                        ins = e.matmul(pm[pi][0:ncols, :], lhsT=wbf[wi][:, kc, 0:ncols],
                                       rhs=c.actT[:, kc, g * 512:(g + 1) * 512],
                                       start=(kc == 0), stop=(kc == 7))
                    return ins
                sc.op("pe", mm, reads=[r_w[wi]] + c.r_actT[g * 4:(g + 1) * 4], writes=[r_pm[pi]])
                en = "act" if (g % 2 == 0) else "dve"
                if en == "act":
                    sc.op("act", lambda e: e.copy(out=dst[part0:part0 + ncols, foff + g * 512:foff + (g + 1) * 512],
                                                  in_=pm[pi][0:ncols, :]),
                          reads=[r_pm[pi]], writes=[r_dst])
                else:
                    sc.op("dve", lambda e: e.tensor_copy(out=dst[part0:part0 + ncols, foff + g * 512:foff + (g + 1) * 512],
                                                         in_=pm[pi][0:ncols, :]),
                          reads=[r_pm[pi]], writes=[r_dst])

        def raw_rows(col0, nblk, s_dst, r_dst):
            for bi in range(nblk):
                b = state["row"] % 2
                state["row"] += 1
                row_block(col0 + bi * 128, 128, acc[b], r_acc[b])
                if bi % 2 == 0:
                    sc.op("act", lambda e: e.copy(out=yb[b][:], in_=acc[b][:]), reads=[r_acc[b]], writes=[r_yb[b]])
                else:
                    sc.op("dve", lambda e: e.tensor_copy(out=yb[b][:], in_=acc[b][:]), reads=[r_acc[b]], writes=[r_yb[b]])
                sc.dma("sp", s_dst[bi], yb[b][:], reads=[r_yb[b]], writes=[r_dst[bi]])
        raw_rows(O_DQ, 8, c.s_dq, c.r_dq)
        raw_rows(O_DK, 2, c.s_dk, c.r_dk)
        raw_rows(O_IQ, 4, c.s_iq, c.r_iq)

        for bi in range(8):
            b = state["row"] % 2
            state["row"] += 1
            row_block(O_GB + bi * 128, 128, acc[b], r_acc[b])
            sc.op("act", lambda e: e.activation(out=yb[b][:], in_=acc[b][:], func=AF.Sigmoid,
                                                bias=gbias[:, 8 + bi:9 + bi]),
                  reads=[r_acc[b], r_cst], writes=[r_yb[b]])
            sc.dma("sp", c.s_gb[bi], yb[b][:], reads=[r_yb[b]], writes=[c.r_gb[bi]])

        for bi in range(8):
            row_block(O_Z + bi * 128, 128, acc[0], r_acc[0])
            row_block(O_GA + bi * 128, 128, acc[1], r_acc[1])
            sc.op("act", lambda e: e.activation(out=rn[:], in_=acc[1][:], func=AF.Sigmoid,
                                                bias=gbias[:, bi:bi + 1]),
                  reads=[r_acc[1], r_cst], writes=[r_rn])
            sc.op("act", lambda e: e.activation(out=acc[1][:], in_=acc[0][:], func=AF.Silu),
                  reads=[r_acc[0]], writes=[r_acc[1]])
            sc.op("dve", lambda e: e.scalar_tensor_tensor(out=yb[0][:], in0=acc[1][:], scalar=gng[:, 0:1],
                                                          in1=rn[:], op0=ALU.mult, op1=ALU.mult),
                  reads=[r_acc[1], r_rn, r_cst], writes=[r_yb[0]])
            sc.dma("sp", c.s_fa[bi], yb[0][:], reads=[r_yb[0]], writes=[c.r_fa[bi]])

        NTK = 344
        wtok = sb("wtok", [128, 8, NTK], BF16)
        r_wtok = Reg()
        for (o, n, src) in ((0, 256, O_DV), (256, 64, O_IK), (320, 8, O_IW), (328, 8, O_A), (336, 8, O_B)):
            sc.dma("pool", wtok[:, :, o:o + n], wv[:, :, src:src + n], writes=[r_wtok])
        alog = sb("alog", [128, 8], F32)
        dtb = sb("dtb", [128, 8], F32)
        ikg = sb("ikg", [128, 64], F32)
        sc.dma("sp", alog[:], c.gdn_a_log.partition_broadcast(128), writes=[r_cst])
        sc.dma("sp", dtb[:], c.gdn_dt_bias.partition_broadcast(128), writes=[r_cst])
        sc.dma("sp", ikg[:], c.idx_k_norm_g.partition_broadcast(128), writes=[r_cst])
        sc.op("act", lambda e: e.activation(out=alog[:], in_=alog[:], func=AF.Exp), reads=[r_cst], writes=[r_cst])
        sc.op("dve", lambda e: e.tensor_scalar(out=alog[:], in0=alog[:], scalar1=-1.0, scalar2=None, op0=ALU.mult),
              reads=[r_cst], writes=[r_cst])
        vt = [sb("vt%d" % i, [128, 256], BF16) for i in range(2)]
        tk = [sb("tk%d" % i, [128, 32], F32) for i in range(2)]
        ikx = [sb("ikx%d" % i, [128, 64], F32) for i in range(2)]
        ikb = [sb("ikb%d" % i, [128, 128], BF16) for i in range(2)]
        ikT = [sb("ikT%d" % i, [128, 128], BF16) for i in range(2)]
        st = [sb("st%d" % i, [128, 4], F32) for i in range(2)]
        tmp8 = [sb("tmp8%d" % i, [128, 16], F32) for i in range(2)]
        tp = [sb("tp%d" % i, [128, 88], F32) for i in range(2)]
        r_tp = regs(2)
        pT = ps("pT", [128, 128], BF16)
        r_pT = Reg()
        r_vt, r_tk, r_ikx, r_ikb, r_ikT, r_st, r_t8 = regs(2), regs(2), regs(2), regs(2), regs(2), regs(2), regs(2)
        tok_pending = []

        def tok_flush(keep):
            while len(tok_pending) > keep:
                t = tok_pending.pop(0)
                b = t % 2
                sc.op("pe", lambda e: e.transpose(out=pT[:], in_=ikb[b][:], identity=c.ident[:]),
                      reads=[r_ikb[b], c.r_const], writes=[r_pT])
                sc.op("act", lambda e: e.copy(out=ikT[b][:], in_=pT[:]), reads=[], writes=[r_ikT[b]], excl=[r_pT])
                sc.dma("sp", c.s_ik[:, t * 128:(t + 1) * 128], ikT[b][:], reads=[r_ikT[b]], writes=[c.r_ik])

        def tok_tile(t):
            b = t % 2
            tok_flush(1)
            pi = state["pm"] % 4
            state["pm"] += 1
            P = pm[pi]

            def mm(e):
                for kc in range(8):
                    ins = e.matmul(P[:, 0:NTK], lhsT=c.actT[:, kc, t * 128:(t + 1) * 128], rhs=wtok[:, kc, :],
                                   start=(kc == 0), stop=(kc == 7))
                return ins
            sc.op("pe", mm, reads=[c.r_actT[t], r_wtok], writes=[r_pm[pi]])
            sc.op("act", lambda e: e.copy(out=vt[b][:], in_=P[:, 0:256]), reads=[], writes=[r_vt[b]], excl=[r_pm[pi]])
            sc.dma("sp", c.s_v[t * 128:(t + 1) * 128, :], vt[b][:], reads=[r_vt[b]], writes=[c.r_v])
            X = tp[b]
            sc.op("dve", lambda e: e.tensor_copy(out=X[:], in_=P[:, 256:344]), reads=[], writes=[r_tp[b]], excl=[r_pm[pi]])
            sc.op("dve", lambda e: e.tensor_reduce(out=st[b][:, 0:1], in_=X[:, 0:64], axis=AX.X, op=ALU.add),
                  reads=[r_tp[b]], writes=[r_st[b]])
            sc.op("dve", lambda e: e.tensor_scalar(out=st[b][:, 0:1], in0=st[b][:, 0:1], scalar1=-1.0 / 64,
                                                   scalar2=None, op0=ALU.mult),
                  reads=[r_st[b]], writes=[r_st[b]])
            sc.op("dve", lambda e: e.tensor_scalar(out=ikx[b][:], in0=X[:, 0:64], scalar1=st[b][:, 0:1],
                                                   scalar2=None, op0=ALU.add),
                  reads=[r_tp[b], r_st[b]], writes=[r_ikx[b]])
            sc.op("act", lambda e: e.activation(out=ikb[b][:, 0:64], in_=ikx[b][:], func=AF.Square,
                                                accum_out=st[b][:, 1:2]),
                  reads=[r_ikx[b]], writes=[r_ikb[b], r_st[b]])
            sc.op("act", lambda e: e.activation(out=st[b][:, 2:3], in_=st[b][:, 1:2], func=AF.Ln, bias=EPS,
                                                scale=1.0 / 64),
                  reads=[r_st[b]], writes=[r_st[b]])
            sc.op("act", lambda e: e.activation(out=st[b][:, 2:3], in_=st[b][:, 2:3], func=AF.Exp, scale=-0.5),
                  reads=[r_st[b]], writes=[r_st[b]])
            for hf in range(2):
                sc.op("dve", lambda e: e.scalar_tensor_tensor(out=ikb[b][:, hf * 64:(hf + 1) * 64], in0=ikx[b][:],
                                                              scalar=st[b][:, 2:3], in1=ikg[:],
                                                              op0=ALU.mult, op1=ALU.mult),
                      reads=[r_ikx[b], r_st[b], r_cst], writes=[r_ikb[b]])
            tok_pending.append(t)
            T8 = tmp8[b]
            sc.op("dve", lambda e: e.tensor_tensor(out=T8[:, 0:8], in0=X[:, 72:80], in1=dtb[:], op=ALU.add),
                  reads=[r_tp[b], r_cst], writes=[r_t8[b]])
            sc.op("act", lambda e: e.activation(out=T8[:, 0:8], in_=T8[:, 0:8], func=AF.Exp),
                  reads=[r_t8[b]], writes=[r_t8[b]])
            sc.op("act", lambda e: e.activation(out=T8[:, 0:8], in_=T8[:, 0:8], func=AF.Ln, bias=1.0),
                  reads=[r_t8[b]], writes=[r_t8[b]])
            sc.op("dve", lambda e: e.tensor_tensor(out=tk[b][:, 0:8], in0=T8[:, 0:8], in1=alog[:], op=ALU.mult),
                  reads=[r_t8[b], r_cst], writes=[r_tk[b]])
            sc.op("act", lambda e: e.activation(out=T8[:, 8:16], in_=X[:, 80:88], func=AF.Exp, scale=-1.0),
                  reads=[r_tp[b]], writes=[r_t8[b]])
            sc.op("dve", lambda e: e.tensor_scalar(out=T8[:, 8:16], in0=T8[:, 8:16], scalar1=1.0, scalar2=None,
                                                   op0=ALU.add),
                  reads=[r_t8[b]], writes=[r_t8[b]])
            sc.op("dve", lambda e: e.reciprocal(out=tk[b][:, 8:16], in_=T8[:, 8:16]),
                  reads=[r_t8[b]], writes=[r_tk[b]])
            sc.op("act", lambda e: e.activation(out=tk[b][:, 16:24], in_=X[:, 64:72], func=AF.Abs),
                  reads=[r_tp[b]], writes=[r_tk[b]])
            sc.op("act", lambda e: e.activation(out=tk[b][:, 24:32], in_=X[:, 64:72], func=AF.Sign),
                  reads=[r_tp[b]], writes=[r_tk[b]])
            sc.dma("sp", c.s_tok[t * 128:(t + 1) * 128, :], tk[b][:], reads=[r_tk[b]], writes=[c.r_tok])
        pn2 = [ps("pn%d" % i, [128, 512]) for i in range(2)]
        r_pn2 = regs(2)
        pk = ps("pk", [128, 512])
        r_pk = Reg()
        state["pn"] = 0
        oneh = sb("oneh", [128, NG, NG], BF16)
        sel = sb("sel", [NG, NG, 128], F32)
        rk = sb("rk", [NG, 512], F32)
        r_rk = Reg()
        sc.op("dve", lambda e: e.memset(oneh[:], 0.0), writes=[r_cst])
        for g_ in range(NG):
            sc.op("dve", lambda e: e.memset(oneh[:, g_, g_:g_ + 1], 1.0), reads=[r_cst], writes=[r_cst])
        selv = sel[:].rearrange("p g c -> p (g c)")
        sc.op("pool", lambda e: e.memset(selv, 1.0), reads=[r_cst], writes=[r_cst])
        sc.op("pool", lambda e: e.affine_select(out=selv, in_=selv, pattern=[[1, NG * 128]], compare_op=ALU.is_ge,
                                                fill=0.0, base=0, channel_multiplier=-128), reads=[r_cst], writes=[r_cst])
        sc.op("pool", lambda e: e.affine_select(out=selv, in_=selv, pattern=[[-1, NG * 128]], compare_op=ALU.is_ge,
                                                fill=0.0, base=127, channel_multiplier=128), reads=[r_cst], writes=[r_cst])
        preb = [sb("preb%d" % i, [128, 3 + S], BF16) for i in range(2)]
        r_preb = [regs(NG + 1), regs(NG + 1)]
        sqb = [sq, sb("sq2", [128, S], BF16)]
        r_sqg = [regs(NG), regs(NG)]
        diag = [sb("diag%d" % i, [128, 4, 128], BF16) for i in range(2)]
        r_diag = regs(2)
        r_accg = [regs(NG), regs(NG)]
        for b in range(2):
            sc.op("dve", lambda e: e.memset(preb[b][:, 0:3], 0.0), writes=[r_preb[b][NG]])
        tok_next = {"t": 0}
        wmap = {}
        for blk in range(24):
            b = blk % 2
            for j in range(4):
                sc.op("dve", lambda e: e.tensor_scalar(out=diag[b][:, j, :], in0=c.ident[:], scalar1=cw[:, blk, j:j + 1],
                                                       scalar2=None, op0=ALU.mult),
                      reads=[c.r_const, r_cst], writes=[r_diag[b]])
            PF = 4
            if blk == 0:
                for k_ in range(min(PF, 24)):
                    wmap[k_] = state["w"] % NW
                    state["w"] += 1
                    sc.dma("pool", wbf[wmap[k_]][:], wv[:, :, O_QKV + k_ * 128:O_QKV + (k_ + 1) * 128], writes=[r_w[wmap[k_]]])
            if blk + PF < 24:
                k_ = blk + PF
                wmap[k_] = state["w"] % NW
                state["w"] += 1
                sc.dma("pool", wbf[wmap[k_]][:], wv[:, :, O_QKV + k_ * 128:O_QKV + (k_ + 1) * 128], writes=[r_w[wmap[k_]]])
            wi = wmap[blk]
            a = acc[b]
            isv = blk >= 16

            def proj(g):
                pi = state["pm"] % 4
                state["pm"] += 1

                def mm(e):
                    for kc in range(8):
                        ins = e.matmul(pm[pi][:], lhsT=wbf[wi][:, kc, :], rhs=c.actT[:, kc, g * 512:(g + 1) * 512],
                                       start=(kc == 0), stop=(kc == 7))
                    return ins
                sc.op("pe", mm, reads=[r_w[wi]] + c.r_actT[g * 4:(g + 1) * 4], writes=[r_pm[pi]])
                dst = preb[b][:, 3 + g * 512:3 + (g + 1) * 512]
                if g % 2 == 0:
                    sc.op("dve", lambda e: e.tensor_copy(out=dst, in_=pm[pi][:]), reads=[], writes=[r_preb[b][g]], excl=[r_pm[pi]])
                else:
                    sc.op("act", lambda e: e.copy(out=dst, in_=pm[pi][:]), reads=[], writes=[r_preb[b][g]], excl=[r_pm[pi]])

            def conv(g):
                pi = state["pm"] % 4
                state["pm"] += 1

                def mm(e):
                    for j in range(4):
                        ins = e.matmul(pm[pi][:], lhsT=diag[b][:, j, :], rhs=preb[b][:, j + g * 512:j + (g + 1) * 512],
                                       start=(j == 0), stop=(j == 3))
                    return ins
                sc.op("pe", mm, reads=[r_diag[b], r_preb[b][g], r_preb[b][g - 1 if g > 0 else NG]], writes=[r_pm[pi]])
                if isv:
                    sc.op("act", lambda e: e.activation(out=yb[b][:, g * 512:(g + 1) * 512], in_=pm[pi][:], func=AF.Silu),
                          reads=[], writes=[r_yb[b]], excl=[r_pm[pi]])
                else:
                    gs_ = slice(g * 512, (g + 1) * 512)
                    sc.op("act", lambda e: e.activation(out=a[:, gs_], in_=pm[pi][:], func=AF.Silu),
                          reads=[], writes=[r_accg[b][g]], excl=[r_pm[pi]])
                    sc.op("dve", lambda e: e.tensor_tensor(out=sqb[b][:, gs_], in0=a[:, gs_], in1=a[:, gs_], op=ALU.mult),
                          reads=[r_accg[b][g]], writes=[r_sqg[b][g]])
            def tail(tb, tblk, part=0):
                ta = acc[tb]
                if tblk < 16 and part in (0, 1):
                    def pack(e):
                        for g in range(NG):
                            ins = e.matmul(pk[0:NG, :], lhsT=oneh[:, g, :], rhs=sqb[tb][:, g * 512:(g + 1) * 512],
                                           start=(g == 0), stop=(g == NG - 1))
                        return ins
                    sc.op("pe", pack, reads=r_sqg[tb] + [r_cst], writes=[r_pk])
                    sc.op("act", lambda e: e.activation(out=rk[:], in_=pk[0:NG, :], func=AF.Ln, bias=EPS),
                          reads=[], writes=[r_rk], excl=[r_pk])
                    sc.op("act", lambda e: e.activation(out=rk[:], in_=rk[:], func=AF.Exp, scale=-0.5),
                          reads=[], writes=[r_rk])
                if part == 1:
                    return
                if tblk < 16:
                    for g in range(NG):
                        ni = state["pn"] % 2
                        state["pn"] += 1
                        gs = slice(g * 512, (g + 1) * 512)
                        sc.op("pe", lambda e: e.matmul(pn2[ni][:], lhsT=sel[:, g, :], rhs=rk[:], start=True, stop=True),
                              reads=[r_rk, r_cst], writes=[r_pn2[ni]])
                        sc.op("dve", lambda e: e.tensor_tensor(out=yb[tb][:, gs], in0=pn2[ni][:], in1=ta[:, gs], op=ALU.mult),
                              reads=[r_accg[tb][g]], writes=[r_yb[tb]], excl=[r_pn2[ni]])
                sc.dma("sp", c.s_gq[tblk], yb[tb][:], reads=[r_yb[tb]], writes=[c.r_gq[tblk]])
                want = ((tblk + 1) * NT) // 24
                while tok_next["t"] < want:
                    tok_tile(tok_next["t"])
                    tok_next["t"] += 1

            proj(0)
            for g in range(NG):
                if g + 1 < NG:
                    proj(g + 1)
                conv(g)
                if NG >= 3 and blk >= 1:
                    if g == 1:
                        tail(1 - b, blk - 1, part=1)
                    if g == 2:
                        tail(1 - b, blk - 1, part=2)
                elif g == min(1, NG - 1) and blk >= 1:
                    tail(1 - b, blk - 1)
            if blk == 23:
                tail(b, blk)
        while tok_next["t"] < NT:
            tok_tile(tok_next["t"])
            tok_next["t"] += 1
        tok_flush(0)
        sc.barrier()


def phase_c(c):
    nc, sc, S, NT = c.nc, c.sc, c.S, c.NT
    H = 8
    with ExitStack() as es:
        sb = lambda n, s_, d: c.sb("pc_" + n, s_, d, es)
        ps = lambda n, s_, d=F32: es.enter_context(nc.psum_tensor("pc_" + n, s_, d))
        U = sb("U", [128, 128], F32)
        Ls = sb("Ls", [128, 128], F32)
        NEGI4 = sb("NEGI4", [128, 4, 128], F32)
        NEGS4 = sb("NEGS4", [128, 4, 128], F32)
        onesf = sb("onesf", [128, 128], F32)
        r_k = Reg()
        P_ = "pool"
        sc.op(P_, lambda e: e.memset(U[:], 1.0), writes=[r_k])
        sc.op(P_, lambda e: e.affine_select(out=U[:], in_=U[:], pattern=[[1, 128]], compare_op=ALU.is_ge, fill=0.0,
                                            base=0, channel_multiplier=-1), reads=[r_k], writes=[r_k])
        sc.op(P_, lambda e: e.memset(U[0:64, 64:128], 0.0), reads=[r_k], writes=[r_k])
        sc.op(P_, lambda e: e.memset(Ls[:], 1.0), reads=[r_k], writes=[r_k])
        sc.op(P_, lambda e: e.affine_select(out=Ls[:], in_=Ls[:], pattern=[[-1, 128]], compare_op=ALU.is_gt, fill=0.0,
                                            base=0, channel_multiplier=1), reads=[r_k], writes=[r_k])
        sc.op(P_, lambda e: e.memset(Ls[64:128, 0:64], 0.0), reads=[r_k], writes=[r_k])
        sc.op(P_, lambda e: e.memset(onesf[:], 1.0), reads=[r_k], writes=[r_k])
        for (dst, src) in ((NEGI4, U), (NEGS4, Ls)):
            sc.op("dve", lambda e: e.tensor_scalar(out=dst[:], in0=src[:].unsqueeze(1).to_broadcast([128, 4, 128]),
                                                   scalar1=-1.0, scalar2=30000.0, op0=ALU.add, op1=ALU.mult),
                  reads=[r_k], writes=[r_k])
        cst = [r_k, c.r_const]

        def t3(name, dt, n=1):
            return [sb("%s%d" % (name, i), [128, H, 128], dt) for i in range(n)]
        qkv = [sb("qkv%d" % i, [128, 24, 128], BF16) for i in range(2)]
        r_qkv = regs(2)
        tk = [sb("tk%d" % i, [128, 32], F32) for i in range(3)]
        r_tk = regs(3)
        fa = t3("fa", BF16, 2)
        r_fa = regs(2)
        GM, GMU = t3("GM", F32)[0], t3("GMU", F32)[0]
        r_GM, r_GMU = Reg(), Reg()
        gam_s, gamT = t3("gams", F32)[0], t3("gamT", F32)[0]
        r_gams, r_gamT = regs(2), regs(2)
        EGb = t3("EGb", F32, 2)
        r_EGb = [regs(2), regs(2)]
        eGt = [sb("eGt%d" % i, [128, 8], F32) for i in range(2)]
        r_eGt = regs(2)
        nbeta = [sb("nbeta%d" % i, [128, 8], F32) for i in range(2)]
        r_nbeta = regs(2)
        eGL = sb("eGL", [128, 8], F32)
        r_eGL = Reg()
        F32R = mybir.dt.float32r if getattr(c, "fp32r", True) else F32
        PA, PB = t3("PA", F32R, 2), t3("PB", F32R, 2)
        r_PA, r_PB = [regs(2), regs(2)], [regs(2), regs(2)]
        Y = t3("Y", F32R, 2)
        r_Y = [regs(2), regs(2)]
        attT, TTb, kd = t3("attT", BF16, 2), t3("TTb", BF16, 2), t3("kd", BF16, 2)
        r_attT, r_TTb, r_kd = [regs(2), regs(2)], [regs(H), regs(H)], [regs(H), regs(H)]
        vtok = t3("vtok", F32, 2)
        r_vtok = regs(2)
        qg = [t3("qglo", BF16, 2), t3("qghi", BF16, 2)]
        kz = [t3("kzlo", BF16, 2), t3("kzhi", BF16, 2)]
        r_qg, r_kz = [regs(2), regs(2)], [regs(2), regs(2)]
        for lst in (qg, kz):
            for a_ in lst:
                for b_ in a_:
                    sc.op(P_, lambda e: e.memset(b_[:], 0.0), writes=[r_k])
        St = t3("St", F32)[0]
        Sb = t3("Sb", BF16)[0]
        r_St, r_Sb = regs(H), regs(H)
        sc.op(P_, lambda e: e.memset(St[:], 0.0), writes=r_St)
        sc.op(P_, lambda e: e.memset(Sb[:], 0.0), writes=r_Sb)
        nR = t3("nR", BF16)[0]
        vnew = t3("vnew", BF16)[0]
        r_nR, r_vnew = regs(H), regs(H)
        oT = t3("oT", F32)[0]
        sq = t3("sq", BF16)[0]
        rs = t3("rs", F32)[0]
        mg = t3("mg", BF16, 2)
        r_oT, r_sq, r_rs, r_mg = Reg(), Reg(), Reg(), regs(2)
        pA = ps("pA", [128, H, 128])
        pB = ps("pB", [128, H, 128])
        pS = ps("pS", [128, H, 128])
        pO = ps("pO", [128, H, 128])
        r_pA, r_pB, r_pS, r_pO = regs(2), regs(2), regs(2), regs(2)
        pA_bf = pA[:].bitcast(BF16)
        pB_bf = pB[:].bitcast(BF16)
        slot = {"n": 0}
        HS = lambda hf: slice(hf * 4, hf * 4 + 4)

        def load(T):
            p = T % 2
            tsl = slice(T * 128, (T + 1) * 128)
            for i3 in range(6):
                sc.dma("sp", qkv[p][:, i3 * 4:(i3 + 1) * 4, :], c.s_gq[i3 * 4:(i3 + 1) * 4, :, tsl].rearrange("b p t -> p b t"),
                       reads=c.r_gq[i3 * 4:(i3 + 1) * 4], writes=[r_qkv[p]])
            for i3 in range(2):
                sc.dma("sp", fa[p][:, i3 * 4:(i3 + 1) * 4, :], c.s_fa[i3 * 4:(i3 + 1) * 4, :, tsl].rearrange("b p t -> p b t"),
                       reads=c.r_fa[i3 * 4:(i3 + 1) * 4], writes=[r_fa[p]])

        def load_tk(T):
            sc.dma("sp", tk[T % 3][:], c.s_tok[T * 128:(T + 1) * 128, :], reads=[c.r_tok], writes=[r_tk[T % 3]])

        def pre_steps(T):
            p = T % 2
            tkT, r_tkT = tk[T % 3], r_tk[T % 3]
            q_ = lambda h: qkv[p][:, h, :]
            k_ = lambda h: qkv[p][:, 8 + h, :]
            v_ = lambda h: qkv[p][:, 16 + h, :]
            gdec_b = tkT[:, 0:8].unsqueeze(2).to_broadcast([128, H, 128])
            steps = []

            def s0():
                sc.op(P_, lambda e: e.tensor_tensor(out=GM[:], in0=Ls[:].unsqueeze(1).to_broadcast([128, H, 128]),
                                                    in1=gdec_b, op=ALU.mult), reads=[r_tkT] + cst, writes=[r_GM])
                sc.op(P_, lambda e: e.tensor_tensor(out=GMU[:], in0=U[:].unsqueeze(1).to_broadcast([128, H, 128]),
                                                    in1=gdec_b, op=ALU.mult), reads=[r_tkT] + cst, writes=[r_GMU])
                sc.op("dve", lambda e: e.tensor_scalar(out=nbeta[p][:], in0=tkT[:, 8:16], scalar1=-1.0, scalar2=None,
                                                       op0=ALU.mult), reads=[r_tkT], writes=[r_nbeta[p]])
            steps.append(s0)

            def s1():
                for hf in range(2):
                    def mmD(e):
                        e.matmul(pA[:, HS(hf), :], lhsT=U[:], rhs=GM[:, HS(hf), :], start=True, stop=False)
                        return e.matmul(pA[:, HS(hf), :], lhsT=c.identf[:], rhs=NEGS4[:], start=False, stop=True)
                    sc.op("pe", mmD, reads=[r_GM] + cst, writes=[r_pA[hf]])

                    def mmDT(e):
                        e.matmul(pB[:, HS(hf), :], lhsT=Ls[:], rhs=GMU[:, HS(hf), :], start=True, stop=False)
                        return e.matmul(pB[:, HS(hf), :], lhsT=c.identf[:], rhs=NEGI4[:], start=False, stop=True)
                    sc.op("pe", mmDT, reads=[r_GMU] + cst, writes=[r_pB[hf]])
                    sc.op("act", lambda e: e.activation(out=gam_s[:, HS(hf), :], in_=pA[:, HS(hf), :], func=AF.Exp),
                          reads=[], writes=[r_gams[hf]], excl=[r_pA[hf]])
                    sc.op("act", lambda e: e.activation(out=gamT[:, HS(hf), :], in_=pB[:, HS(hf), :], func=AF.Exp),
                          reads=[], writes=[r_gamT[hf]], excl=[r_pB[hf]])
            steps.append(s1)

            def s2():
                for hf in range(2):
                    sc.op("pe", lambda e: e.matmul(pA[:, HS(hf), :], lhsT=onesf[:], rhs=GMU[:, HS(hf), :],
                                                   start=True, stop=True), reads=[r_GMU] + cst, writes=[r_pA[hf]])
                    sc.op("act", lambda e: e.activation(out=EGb[p][:, HS(hf), :], in_=pA[:, HS(hf), :], func=AF.Exp),
                          reads=[], writes=[r_EGb[p][hf]], excl=[r_pA[hf]])
                sc.op("pe", lambda e: e.matmul(pB[:, 0, 0:8], lhsT=U[:], rhs=tkT[:, 0:8], start=True, stop=True),
                      reads=[r_tkT] + cst, writes=[r_pB[0]])
                sc.op("act", lambda e: e.activation(out=eGt[p][:], in_=pB[:, 0, 0:8], func=AF.Exp),
                      reads=[], writes=[r_eGt[p]], excl=[r_pB[0]])
                sc.op("dve", lambda e: e.tensor_copy(out=eGL[0:64, :], in_=gamT[0:64, :, 63]),
                      reads=r_gamT, writes=[r_eGL])
                sc.op("dve", lambda e: e.tensor_copy(out=eGL[64:128, :], in_=gamT[64:128, :, 127]),
                      reads=r_gamT, writes=[r_eGL])
            steps.append(s2)

            def s3():
                for h in range(H):
                    hf = h // 4
                    sc.op("pe", lambda e: e.matmul(pA[:, h, :], lhsT=k_(h), rhs=k_(h), start=True, stop=True),
                          reads=[r_qkv[p]], writes=[r_pA[hf]])
                    sc.op("pe", lambda e: e.matmul(pB[:, h, :], lhsT=k_(h), rhs=q_(h), start=True, stop=True),
                          reads=[r_qkv[p]], writes=[r_pB[hf]])
                for h in range(H):
                    hf = h // 4
                    sc.op("dve", lambda e: e.scalar_tensor_tensor(out=PA[0][:, h, :], in0=pA[:, h, :],
                                                                  scalar=tkT[:, 8 + h:9 + h], in1=gam_s[:, h, :],
                                                                  op0=ALU.mult, op1=ALU.mult),
                          reads=[r_tkT, r_gams[hf]], writes=[r_PA[0][hf]], excl=[r_pA[hf]])
                for hf in range(2):
                    sc.op("dve", lambda e: e.scalar_tensor_tensor(out=attT[p][:, HS(hf), :], in0=pB[:, HS(hf), :],
                                                                  scalar=128.0 ** -0.5, in1=gamT[:, HS(hf), :],
                                                                  op0=ALU.mult, op1=ALU.mult),
                          reads=[r_gamT[hf]], writes=[r_attT[p][hf]], excl=[r_pB[hf]])
            steps.append(s3)

            def s4():
                for h in range(H):
                    sc.op("pe", lambda e: e.transpose(out=pA[:, h, :], in_=PA[0][:, h, :].bitcast(F32), identity=c.identf[:]),
                          reads=[r_PA[0][h // 4]] + cst, writes=[r_pA[h // 4]])
                for hf in range(2):
                    sc.op("act", lambda e: e.copy(out=PB[0][:, HS(hf), :], in_=pA[:, HS(hf), :]),
                          reads=[], writes=[r_PB[0][hf]], excl=[r_pA[hf]])
                    sc.op("dve", lambda e: e.scalar_tensor_tensor(
                        out=Y[0][:, HS(hf), :], in0=pA[:, HS(hf), :], scalar=-1.0,
                        in1=c.identf[:].unsqueeze(1).to_broadcast([128, 4, 128]), op0=ALU.mult, op1=ALU.add),
                        reads=cst, writes=[r_Y[0][hf]], excl=[r_pA[hf]])
            steps.append(s4)

            R_ = lambda ap: ap
            F_ = lambda ap: ap.bitcast(F32)

            def level(k):
                def f():
                    a, b = (k - 1) % 2, k % 2
                    for h in range(H):
                        hf = h // 4
                        sc.op("pe", lambda e: e.matmul(pA[:, h, :], lhsT=R_(PB[a][:, h, :]), rhs=R_(PA[a][:, h, :]),
                                                       start=True, stop=True),
                              reads=[r_PA[a][hf], r_PB[a][hf]], writes=[r_pA[hf]])
                        if k < 5:
                            sc.op("pe", lambda e: e.matmul(pB[:, h, :], lhsT=R_(PA[a][:, h, :]), rhs=R_(PB[a][:, h, :]),
                                                           start=True, stop=True),
                                  reads=[r_PA[a][hf], r_PB[a][hf]], writes=[r_pB[hf]])
                    for hf in range(2):
                        sc.op("act", lambda e: e.copy(out=PA[b][:, HS(hf), :], in_=pA[:, HS(hf), :]),
                              reads=[], writes=[r_PA[b][hf]], excl=[r_pA[hf]])
                        if k < 5:
                            sc.op("dve", lambda e: e.tensor_copy(out=PB[b][:, HS(hf), :], in_=pB[:, HS(hf), :]),
                                  reads=[], writes=[r_PB[b][hf]], excl=[r_pB[hf]])
                    for h in range(H):
                        hf = h // 4
                        sc.op("pe", lambda e: e.matmul(pA[:, h, :], lhsT=R_(PA[b][:, h, :]), rhs=R_(Y[a][:, h, :]),
                                                       start=True, stop=True),
                              reads=[r_PA[b][hf], r_Y[a][hf]], writes=[r_pA[hf]])
                    for hf in range(2):
                        sc.op("dve", lambda e: e.tensor_tensor(out=Y[b][:, HS(hf), :], in0=pA[:, HS(hf), :],
                                                               in1=Y[a][:, HS(hf), :].bitcast(F32), op=ALU.add),
                              reads=[r_Y[a][hf]], writes=[r_Y[b][hf]], excl=[r_pA[hf]])
                return f
            for k in range(1, 6):
                steps.append(level(k))

            def s5():
                for h in range(H):
                    sc.op("act", lambda e: e.activation(out=TTb[p][:, h, :], in_=Y[1][:, h, :].bitcast(F32),
                                                        func=AF.Copy, scale=nbeta[p][:, h:h + 1]),
                          reads=[r_Y[1][h // 4], r_nbeta[p]], writes=[r_TTb[p][h]])
                for h in range(H):
                    sc.op("pe", lambda e: e.transpose(out=pA_bf[:, h, 0:128], in_=k_(h), identity=c.ident[:]),
                          reads=[r_qkv[p]] + cst, writes=[r_pA[h // 4]])
                    sc.op("pe", lambda e: e.transpose(out=pB_bf[:, h, 0:128], in_=v_(h), identity=c.ident[:]),
                          reads=[r_qkv[p]] + cst, writes=[r_pB[h // 4]])
                for h in range(H):
                    sc.op("dve", lambda e: e.tensor_scalar(out=kd[p][:, h, :], in0=pA_bf[:, h, 0:128],
                                                           scalar1=eGL[:, h:h + 1], scalar2=None, op0=ALU.mult),
                          reads=[r_eGL], writes=[r_kd[p][h]], excl=[r_pA[h // 4]])
                sc.op("act", lambda e: e.copy(out=vtok[p][:], in_=pB_bf[:, :, 0:128]), reads=[], writes=[r_vtok[p]], excl=r_pB)
                for lo in range(2):
                    cs = slice(lo * 64, lo * 64 + 64)
                    sc.op("dve", lambda e: e.scalar_tensor_tensor(out=qg[lo][p][:, :, cs], in0=qkv[p][:, 0:8, cs],
                                                                  scalar=128.0 ** -0.5, in1=EGb[p][:, :, cs],
                                                                  op0=ALU.mult, op1=ALU.mult),
                          reads=[r_qkv[p]] + r_EGb[p], writes=[r_qg[lo][p]])
                    sc.op(P_, lambda e: e.tensor_copy(out=kz[lo][p][:, :, cs], in_=qkv[p][:, 8:16, cs]),
                          reads=[r_qkv[p]], writes=[r_kz[lo][p]])
            steps.append(s5)
            return steps

        def scan_steps(T):
            p = T % 2
            steps = []
            for ch in range(2):
                rr = slice(ch * 64, ch * 64 + 64)

                def st_p1(ch=ch, rr=rr):
                    for hf in range(2):
                        def pe(e):
                            for h in range(hf * 4, hf * 4 + 4):
                                ins = e.matmul(pS[:, h, :], lhsT=kz[ch][p][:, h, :], rhs=Sb[:, h, :], start=True, stop=True)
                            return ins
                        sc.op("pe", pe, reads=[r_kz[ch][p]] + r_Sb[hf * 4:hf * 4 + 4], writes=[r_pS[hf]])
                    for hf in range(2):
                        for h in range(hf * 4, hf * 4 + 4):
                            sc.op("dve", lambda e: e.scalar_tensor_tensor(out=nR[rr, h, :], in0=pS[rr, h, :],
                                                                          scalar=eGt[p][rr, h:h + 1], in1=vtok[p][rr, h, :],
                                                                          op0=ALU.mult, op1=ALU.subtract),
                                  reads=[r_eGt[p], r_vtok[p]], writes=[r_nR[h]], excl=[r_pS[hf]])
                steps.append(st_p1)

                def st_vn(ch=ch, rr=rr):
                    for hf in range(2):
                        def pe(e):
                            for h in range(hf * 4, hf * 4 + 4):
                                ins = e.matmul(pS[:, h, :], lhsT=TTb[p][rr, h, :], rhs=nR[rr, h, :], start=True, stop=True)
                            return ins
                        sc.op("pe", pe, reads=r_TTb[p][hf * 4:hf * 4 + 4] + r_nR[hf * 4:hf * 4 + 4], writes=[r_pS[hf]])
                    for hf in range(2):
                        sc.op("act", lambda e: e.copy(out=vnew[rr, HS(hf), :], in_=pS[rr, HS(hf), :]),
                              reads=[], writes=r_vnew[hf * 4:hf * 4 + 4] + [], excl=[r_pS[hf]])
                steps.append(st_vn)

                def st_o(ch=ch, rr=rr):
                    for hf in range(2):
                        def pe(e):
                            for h in range(hf * 4, hf * 4 + 4):
                                first = (ch == 0 and h == hf * 4)
                                e.matmul(pO[:, h, :], lhsT=Sb[:, h, :], rhs=qg[ch][p][:, h, :], start=first, stop=False,
                                         skip_group_check=True)
                                e.matmul(pO[:, h, :], lhsT=vnew[rr, h, :], rhs=attT[p][rr, h, :], start=False,
                                         stop=(ch == 1), skip_group_check=True)
                                ins = e.matmul(pS[:, h, :], lhsT=kd[p][rr, h, :], rhs=vnew[rr, h, :], start=True, stop=True)
                            return ins
                        sc.op("pe", pe, reads=r_Sb[hf * 4:hf * 4 + 4] + [r_qg[ch][p], r_attT[p][hf]] + r_vnew[hf * 4:hf * 4 + 4]
                              + r_kd[p][hf * 4:hf * 4 + 4], writes=[r_pO[hf], r_pS[hf]])
                    col = ch * 64 + 63
                    for hf in range(2):
                        for h in range(hf * 4, hf * 4 + 4):
                            sc.op("dve", lambda e: e.scalar_tensor_tensor(out=St[:, h, :], in0=St[:, h, :],
                                                                          scalar=EGb[p][:, h, col:col + 1], in1=pS[:, h, :],
                                                                          op0=ALU.mult, op1=ALU.add),
                                  reads=[r_EGb[p][hf]], writes=[r_St[h]], excl=[r_pS[hf]])
                        sc.op("act", lambda e: e.copy(out=Sb[:, HS(hf), :], in_=St[:, HS(hf), :]),
                              reads=r_St[hf * 4:hf * 4 + 4], writes=r_Sb[hf * 4:hf * 4 + 4])
                steps.append(st_o)

            def post():
                for hf in range(2):
                    sc.op("act", lambda e: e.copy(out=oT[:, HS(hf), :], in_=pO[:, HS(hf), :]),
                          reads=[], writes=[r_oT], excl=[r_pO[hf]])
                sc.op(P_, lambda e: e.tensor_tensor(out=sq[:], in0=oT[:], in1=oT[:], op=ALU.mult),
                      reads=[r_oT], writes=[r_sq])
                for hf in range(2):
                    sc.op("pe", lambda e: e.matmul(pO[:, HS(hf), :], lhsT=c.ones_bf[:], rhs=sq[:, HS(hf), :],
                                                   start=True, stop=True),
                          reads=[r_sq] + cst, writes=[r_pO[hf]])
                    sc.op("act", lambda e: e.activation(out=rs[:, HS(hf), :], in_=pO[:, HS(hf), :], func=AF.Ln,
                                                        bias=EPS, scale=1.0 / 128),
                          reads=[], writes=[r_rs], excl=[r_pO[hf]])
                sc.op("act", lambda e: e.activation(out=rs[:], in_=rs[:], func=AF.Exp, scale=-0.5),
                      reads=[r_rs], writes=[r_rs])
                sc.op("dve", lambda e: e.tensor_tensor(out=oT[:], in0=oT[:], in1=rs[:], op=ALU.mult),
                      reads=[r_oT, r_rs], writes=[r_oT])
                sc.op("dve", lambda e: e.tensor_tensor(out=mg[p][:], in0=oT[:], in1=fa[p][:], op=ALU.mult),
                      reads=[r_oT, r_fa[p]], writes=[r_mg[p]])
                for i3 in range(2):
                    sc.dma("sp", c.s_mg[i3 * 4:(i3 + 1) * 4, :, T * 128:(T + 1) * 128].rearrange("b p t -> p b t"),
                           mg[p][:, i3 * 4:(i3 + 1) * 4, :], reads=[r_mg[p]], writes=[c.r_mg[T]])
            steps.append(post)
            return steps

        load_tk(0)
        load(0)
        P0 = pre_steps(0)
        for f in P0:
            f()
        nxt = None
        if NT > 1:
            load_tk(1)
            nxt = pre_steps(1)
            nxt[0]()
        for T in range(NT):
            a = scan_steps(T)
            b = []
            nn = None
            if T + 1 < NT:
                load(T + 1)
                b = nxt[1:]
                if T + 2 < NT:
                    load_tk(T + 2)
                    nn = pre_steps(T + 2)
            n = max(len(a), len(b))
            for i in range(n):
                if i < len(b):
                    b[i]()
                if i == 1 and nn is not None:
                    nn[0]()
                if i < len(a):
                    a[i]()
            nxt = nn
        sc.barrier()


def phase_d(c):
    nc, sc, S, NT = c.nc, c.sc, c.S, c.NT
    KTOP = min(TOPK, S // 4)
    NIT = 14
    BIG = 1.0e30
    with ExitStack() as es:
        sb = lambda n, s_, d: c.sb("pd_" + n, s_, d, es)
        ps = lambda n, s_, d=F32: es.enter_context(nc.psum_tensor("pd_" + n, s_, d))
        kT = sb("kT", [128, 2, S], BF16)
        vtok = sb("vtok", [128, NT, 256], BF16)
        ikT = sb("ikT", [128, S], BF16)
        wout = sb("wout", [128, 8, D], BF16)
        negu = sb("negu", [128, 128], F32)
        posu = sb("posu", [128, 128], F32)
        pow2 = sb("pow2", [128, NIT + 1], F32)
        r_res = Reg()
        for g in range(2):
            sc.dma("sp", kT[:, g, :], c.s_dk[g], reads=c.r_dk, writes=[r_res])
        for t0 in range(0, NT, 8):
            t1 = min(NT, t0 + 8)
            sc.dma("sp", vtok[:, t0:t1, :], c.s_v[t0 * 128:t1 * 128, :].rearrange("(n p) c -> p n c", p=128),
                   reads=[c.r_v], writes=[r_res])
        sc.dma("sp", ikT[:], c.s_ik, reads=[c.r_ik], writes=[r_res])
        sc.dma("pool", wout[:], c.w_out.rearrange("(kc p) n -> p kc n", p=128), writes=[r_res])
        sc.op("pool", lambda e: e.memset(negu[:], -BIG), writes=[r_res])
        sc.op("pool", lambda e: e.affine_select(out=negu[:], in_=negu[:], pattern=[[1, 128]], compare_op=ALU.is_gt,
                                                fill=0.0, base=0, channel_multiplier=-1), reads=[r_res], writes=[r_res])
        sc.op("dve", lambda e: e.tensor_scalar(out=posu[:], in0=negu[:], scalar1=-1.0, scalar2=None, op0=ALU.mult),
              reads=[r_res], writes=[r_res])
        for n in range(NIT + 1):
            val = 2.0 ** -(n + 1) if n < NIT else 1.25 * 2.0 ** -NIT
            sc.op("pool", lambda e: e.memset(pow2[:, n:n + 1], val), reads=[r_res], writes=[r_res])
        scoreb = [sb("score%d" % i, [128, S], F32) for i in range(2)]
        r_scoreb = regs(2)
        junk = sb("junk", [128, S], BF16)
        r_junk = Reg()
        junk2 = sb("junk2", [128, S // 2], BF16)
        r_junk2, r_nm, r_sg, r_cn, r_st, r_lo = Reg(), Reg(), Reg(), Reg(), Reg(), Reg()
        mask = sb("mask", [128, S], BF16)
        r_mask = Reg()
        mbT = [sb("mbT%d" % i, [128, NT, 128], BF16) for i in range(2)]
        r_mbT = regs(2)
        tmp = [sb("tmp%d" % i, [128, 512], F32) for i in range(2)]
        r_tmp = regs(2)
        dmin = sb("dmin", [128, 128], F32)
        r_dmin = Reg()
        bs = sb("bs", [128, 8], F32)
        r_bs = Reg()
        W = sb("W", [128, NIT + 1], F32)
        r_W = Reg()
        qT = [sb("qT%d" % i, [128, 8, 128], BF16) for i in range(2)]
        iq = [sb("iq%d" % i, [128, 4, 128], BF16) for i in range(3)]
        tok = [sb("tok%d" % i, [128, 32], F32) for i in range(3)]
        gb = [sb("gb%d" % i, [128, 8, 128], BF16) for i in range(2)]
        mgb = [sb("mgb%d" % i, [128, 8, 128], BF16) for i in range(2)]
        xt = [sb("xt%d" % i, [128, D], F32) for i in range(2)]
        r_qT, r_iq, r_tok, r_gb, r_mgb, r_xt = regs(2), regs(3), regs(3), regs(2), regs(2), regs(2)
        pT_ = [sb("pT%d" % i, [128, 2, 4, 128], BF16) for i in range(2)]
        r_pTs = regs(2)
        rden = sb("rden", [128, 4, 128], F32)
        t1_ = sb("t1", [128, 4, 128], F32)
        r_rden, r_t1 = Reg(), Reg()
        mixT = sb("mixT", [128, 8, 128], BF16)
        r_mix = regs(2)
        ht = [sb("ht%d" % i, [128, D], F32) for i in range(2)]
        r_ht = regs(2)
        psc = [ps("psc%d" % i, [128, 512]) for i in range(2)]
        plg = [ps("plg%d" % i, [128, 2, 4, 128]) for i in range(2)]
        po = [ps("po0", [128, 4, 128])] * 2
        pdn = [ps("pdn0", [128, 4, 128])] * 2
        r_psc, r_plg = regs(2), regs(2)
        r_po = [Reg()] * 2
        r_pdn = [Reg()] * 2
        psc_bf = [p_[:].bitcast(BF16) for p_ in psc]
        cnt = {"psc": 0, "plg": 0, "tmp": 0, "pT": 0}

        def load_b(qb):
            p = qb % 2
            tsl = slice(qb * 128, (qb + 1) * 128)
            for i3 in range(2):
                sc.dma("sp", qT[p][:, i3 * 4:(i3 + 1) * 4, :], c.s_dq[i3 * 4:(i3 + 1) * 4, :, tsl].rearrange("b p t -> p b t"),
                       reads=c.r_dq, writes=[r_qT[p]])
                sc.dma("sp", gb[p][:, i3 * 4:(i3 + 1) * 4, :], c.s_gb[i3 * 4:(i3 + 1) * 4, :, tsl].rearrange("b p t -> p b t"),
                       reads=c.r_gb, writes=[r_gb[p]])
                sc.dma("sp", mgb[p][:, i3 * 4:(i3 + 1) * 4, :], c.s_mg[i3 * 4:(i3 + 1) * 4, :, tsl].rearrange("b p t -> p b t"),
                       reads=[c.r_mg[qb]], writes=[r_mgb[p]])
            sc.dma("sp", xt[p][:], c.x[tsl, :], writes=[r_xt[p]])

        def load_a(qb):
            p3 = qb % 3
            tsl = slice(qb * 128, (qb + 1) * 128)
            sc.dma("sp", iq[p3][:], c.s_iq[:, :, tsl].rearrange("b p t -> p b t"), reads=c.r_iq, writes=[r_iq[p3]])
            sc.dma("sp", tok[p3][:], c.s_tok[tsl, :], reads=[c.r_tok], writes=[r_tok[p3]])

        def thread_a1(qb):
            p = qb % 3
            score, r_score = scoreb[qb % 2], r_scoreb[qb % 2]
            L = (qb + 1) * 128
            steps = []
            for c0 in range(0, L, 512):
                w = min(512, L - c0)
                for h in range(8):
                    def unit(c0=c0, w=w, h=h):
                        pi = cnt["psc"] % 2
                        cnt["psc"] += 1
                        ti = cnt["tmp"] % 2
                        cnt["tmp"] += 1
                        hs = slice((h % 2) * 64, (h % 2) * 64 + 64)

                        def pe(e):
                            for o in range(0, w, 512):
                                w2 = min(512, w - o)
                                ins = e.matmul(psc[pi][:, o:o + w2], lhsT=iq[p][hs, h // 2, :], rhs=ikT[hs, c0 + o:c0 + o + w2],
                                               start=True, stop=True)
                            return ins
                        sc.op("pe", pe, reads=[r_iq[p], r_res], writes=[r_psc[pi]])
                        sc.op("act", lambda e: e.activation(out=tmp[ti][:, 0:w], in_=psc[pi][:, 0:w], func=AF.Relu,
                                                            scale=tok[p][:, 16 + h:17 + h]),
                              reads=[r_tok[p]], writes=[r_tmp[ti]], excl=[r_psc[pi]])
                        if h == 0:
                            sc.op("dve", lambda e: e.tensor_scalar(out=score[:, c0:c0 + w], in0=tmp[ti][:, 0:w],
                                                                   scalar1=tok[p][:, 24:25], scalar2=None, op0=ALU.mult),
                                  reads=[r_tmp[ti], r_tok[p]], writes=[r_score])
                        else:
                            sc.op("dve", lambda e: e.scalar_tensor_tensor(out=score[:, c0:c0 + w], in0=tmp[ti][:, 0:w],
                                                                          scalar=tok[p][:, 24 + h:25 + h],
                                                                          in1=score[:, c0:c0 + w], op0=ALU.mult, op1=ALU.add),
                                  reads=[r_tmp[ti], r_tok[p], r_score], writes=[r_score])
                    steps.append(unit)

            return steps

        def thread_a2(qb):
            p = qb % 2
            score, r_score = scoreb[qb % 2], r_scoreb[qb % 2]
            L = (qb + 1) * 128
            steps = []

            def bis_init():
                dsl = slice(qb * 128, L)
                V = "dve"
                sc.op(V, lambda e: e.tensor_tensor(out=dmin[:], in0=score[:, dsl], in1=posu[:], op=ALU.add),
                      reads=[r_score, r_res], writes=[r_dmin])
                sc.op(V, lambda e: e.tensor_reduce(out=bs[:, 0:1], in_=dmin[:], axis=AX.X, op=ALU.min),
                      reads=[r_dmin], writes=[r_bs])
                if qb > 0:
                    sc.op(V, lambda e: e.tensor_reduce(out=bs[:, 5:6], in_=score[:, 0:qb * 128], axis=AX.X, op=ALU.min),
                          reads=[r_score], writes=[r_bs])
                    sc.op(V, lambda e: e.tensor_tensor(out=bs[:, 0:1], in0=bs[:, 0:1], in1=bs[:, 5:6], op=ALU.min),
                          reads=[r_bs], writes=[r_bs])
                sc.op(V, lambda e: e.tensor_tensor(out=score[:, dsl], in0=score[:, dsl], in1=negu[:], op=ALU.add),
                      reads=[r_score, r_res, r_dmin], writes=[r_score])
                sc.op(V, lambda e: e.tensor_reduce(out=bs[:, 1:2], in_=score[:, 0:L], axis=AX.X, op=ALU.max),
                      reads=[r_score], writes=[r_bs])
                sc.op(V, lambda e: e.scalar_tensor_tensor(out=bs[:, 1:2], in0=bs[:, 1:2], scalar=1.0, in1=bs[:, 0:1],
                                                          op0=ALU.add, op1=ALU.subtract), reads=[r_bs], writes=[r_bs])
                sc.op(V, lambda e: e.tensor_scalar(out=W[:], in0=pow2[:], scalar1=bs[:, 1:2], scalar2=None, op0=ALU.mult),
                      reads=[r_bs, r_res], writes=[r_W])
                sc.op(V, lambda e: e.tensor_tensor(out=bs[:, 2:3], in0=bs[:, 0:1], in1=W[:, 0:1], op=ALU.add),
                      reads=[r_bs, r_W], writes=[r_bs, r_lo, r_cn, r_st, r_sg])
            steps.append(bis_init)
            LA = (L // 256) * 128
            LD = L - LA
            thr_c = float(KTOP) - 0.5 * LA
            for n in range(NIT):
                def bis(n=n):
                    V = "dve"
                    if LA > 0:
                        sc.op("act", lambda e: e.activation(out=junk2[:, 0:LA], in_=score[:, LD:L], func=AF.Sign,
                                                            bias=bs[:, 2:3], scale=-1.0, accum_out=bs[:, 7:8]),
                              reads=[r_score, r_bs], writes=[r_junk2, r_sg])
                    sc.op(V, lambda e: e.tensor_scalar(out=junk[:, 0:LD], in0=score[:, 0:LD], scalar1=bs[:, 2:3], scalar2=0.0,
                                                       op0=ALU.is_ge, op1=ALU.add, accum_out=bs[:, 3:4]),
                          reads=[r_score, r_bs], writes=[r_junk, r_cn])
                    if LA > 0:
                        sc.op(V, lambda e: e.scalar_tensor_tensor(out=bs[:, 3:4], in0=bs[:, 7:8], scalar=-0.5, in1=bs[:, 3:4],
                                                                  op0=ALU.mult, op1=ALU.add),
                              reads=[r_sg, r_cn], writes=[r_cn])
                    sc.op(V, lambda e: e.tensor_scalar(out=bs[:, 4:5], in0=bs[:, 3:4], scalar1=thr_c if LA > 0 else float(KTOP),
                                                       scalar2=W[:, n:n + 1], op0=ALU.is_ge, op1=ALU.mult),
                          reads=[r_cn, r_W], writes=[r_st])
                    if n + 1 < NIT:
                        sc.op(V, lambda e: e.scalar_tensor_tensor(out=bs[:, 2:3], in0=bs[:, 4:5], scalar=bs[:, 2:3],
                                                                  in1=W[:, n + 1:n + 2], op0=ALU.add, op1=ALU.subtract),
                              reads=[r_st, r_W], writes=[r_bs])
                    else:
                        sc.op(V, lambda e: e.scalar_tensor_tensor(out=bs[:, 0:1], in0=bs[:, 4:5], scalar=bs[:, 2:3],
                                                                  in1=W[:, NIT:NIT + 1], op0=ALU.add, op1=ALU.subtract),
                              reads=[r_st, r_W, r_bs], writes=[r_lo])
                steps.append(bis)

            def mk_mask():
                sc.op("dve", lambda e: e.tensor_scalar(out=mask[:, 0:L], in0=score[:, 0:L], scalar1=bs[:, 0:1], scalar2=None,
                                                       op0=ALU.is_ge), reads=[r_score, r_bs, r_lo], writes=[r_mask])
            steps.append(mk_mask)
            for k0 in range(0, qb + 1, 8):
                def tr(k0=k0):
                    k1 = min(qb + 1, k0 + 8)
                    pi = cnt["psc"] % 2
                    cnt["psc"] += 1

                    def pe(e):
                        for kb in range(k0, k1):
                            ins = e.transpose(out=psc_bf[pi][:, (kb - k0) * 128:(kb - k0 + 1) * 128],
                                              in_=mask[:, kb * 128:(kb + 1) * 128], identity=c.ident[:])
                        return ins
                    sc.op("pe", pe, reads=[r_mask, c.r_const], writes=[r_psc[pi]])
                    sc.op("act", lambda e: e.activation(out=mbT[p][:, k0:k1, :].rearrange("p a b -> p (a b)"), in_=psc_bf[pi][:, 0:(k1 - k0) * 128],
                                                        func=AF.Identity, bias=-30000.0, scale=30000.0),
                          reads=[], writes=[r_mbT[p]], excl=[r_psc[pi]])
                steps.append(tr)
            return steps

        def thread_b(qb):
            p = qb % 2
            steps = []
            units = [(g, list(range(k0, min(k0 + 2, qb + 1)))) for g in range(2) for k0 in range(0, qb + 1, 2)]
            bufs = {}

            def head(i):
                g, kbs = units[i]
                nk = len(kbs)
                li = cnt["plg"] % 2
                cnt["plg"] += 1
                ti = cnt["pT"] % 2
                cnt["pT"] += 1
                bufs[i] = ti

                def pe(e):
                    for j, kb in enumerate(kbs):
                        e.matmul(plg[li][:, j], lhsT=kT[:, g, kb * 128:(kb + 1) * 128], rhs=qT[p][:, g * 4:g * 4 + 4, :],
                                 start=True, stop=False)
                        ins = e.matmul(plg[li][:, j], lhsT=c.ident[:],
                                       rhs=mbT[p][:, kb, :].unsqueeze(1).to_broadcast([128, 4, 128]),
                                       start=False, stop=True)
                    return ins
                sc.op("pe", pe, reads=[r_res, r_qT[p], r_mbT[p], c.r_const], writes=[r_plg[li]])
                sc.op("act", lambda e: e.activation(out=pT_[ti][:, 0:nk], in_=plg[li][:, 0:nk], func=AF.Exp,
                                                    scale=128.0 ** -0.5),
                      reads=[], writes=[r_pTs[ti]], excl=[r_plg[li]])

            def tail(i):
                g, kbs = units[i]
                ti = bufs[i]

                def pe2(e):
                    for j, kb in enumerate(kbs):
                        e.matmul(po[g][:], lhsT=vtok[:, kb, g * 128:(g + 1) * 128], rhs=pT_[ti][:, j],
                                 start=(kb == 0), stop=(kb == qb))
                        ins = e.matmul(pdn[g][:], lhsT=c.ones_bf[:], rhs=pT_[ti][:, j], start=(kb == 0), stop=(kb == qb))
                    return ins
                sc.op("pe", pe2, reads=[r_res, r_pTs[ti], c.r_const], writes=[r_po[g], r_pdn[g]])

            def fin(g):
                hs = slice(g * 4, g * 4 + 4)
                sc.op("act", lambda e: e.activation(out=rden[:], in_=pdn[g][:], func=AF.Ln), reads=[], writes=[r_rden], excl=[r_pdn[g]])
                sc.op("act", lambda e: e.activation(out=rden[:], in_=rden[:], func=AF.Exp, scale=-1.0), reads=[], writes=[r_rden])
                sc.op("dve", lambda e: e.tensor_tensor(out=t1_[:], in0=po[g][:], in1=rden[:], op=ALU.mult),
                      reads=[r_rden], writes=[r_t1], excl=[r_po[g]])
                sc.op("pool", lambda e: e.tensor_tensor(out=t1_[:], in0=t1_[:], in1=gb[p][:, hs, :], op=ALU.mult),
                      reads=[r_gb[p]], writes=[r_t1])
                sc.op("pool", lambda e: e.tensor_tensor(out=mixT[:, hs, :], in0=t1_[:], in1=mgb[p][:, hs, :], op=ALU.add),
                      reads=[r_t1, r_mgb[p]], writes=[r_mix[g]])

            nu = len(units)

            def mk(i):
                def f():
                    if i + 1 < nu:
                        head(i + 1)
                    tail(i)
                    if units[i][1][-1] == qb:
                        fin(units[i][0])
                return f
            steps.append(lambda: head(0))
            for i in range(nu):
                steps.append(mk(i))

            def wo():
                for dg in range(2):
                    li = cnt["plg"] % 2
                    cnt["plg"] += 1

                    def pe(e):
                        for ec in range(8):
                            ins = e.matmul(plg[li][:, 0].rearrange("p a b -> p (a b)"), lhsT=mixT[:, ec, :],
                                           rhs=wout[:, ec, dg * 512:(dg + 1) * 512], start=(ec == 0), stop=(ec == 7))
                        return ins
                    sc.op("pe", pe, reads=r_mix + [r_res], writes=[r_plg[li]])
                    sc.op("dve", lambda e: e.tensor_tensor(out=ht[p][:, dg * 512:(dg + 1) * 512],
                                                           in0=plg[li][:, 0].rearrange("p a b -> p (a b)"),
                                                           in1=xt[p][:, dg * 512:(dg + 1) * 512], op=ALU.add),
                          reads=[r_xt[p]], writes=[r_ht[p]], excl=[r_plg[li]])
                sc.dma("sp", c.s_h[qb * 128:(qb + 1) * 128, :], ht[p][:], reads=[r_ht[p]], writes=[c.r_h[qb]])
            steps.append(wo)
            return steps

        def merge(a, b):
            na, nb = len(a), len(b)
            out, ia, ib = [], 0, 0
            while ia < na or ib < nb:
                if ib >= nb or (ia < na and ia * nb <= ib * na):
                    out.append(a[ia]); ia += 1
                else:
                    out.append(b[ib]); ib += 1
            return out

        def merge3(lists):
            lists = [l for l in lists if l]
            out = []
            idx = [0] * len(lists)
            tot = sum(len(l) for l in lists)
            while len(out) < tot:
                best, bi = None, -1
                for i, l in enumerate(lists):
                    if idx[i] < len(l):
                        frac = (idx[i] + 0.5) / len(l)
                        if best is None or frac < best:
                            best, bi = frac, i
                out.append(lists[bi][idx[bi]])
                idx[bi] += 1
            return out

        load_a(0)
        load_b(0)
        for f in thread_a1(0):
            f()
        if NT > 1:
            load_a(1)
        for f in merge3([thread_a2(0), thread_a1(1) if NT > 1 else []]):
            f()
        for qb in range(NT):
            lists = [thread_b(qb)]
            if qb + 1 < NT:
                load_b(qb + 1)
                lists.append(thread_a2(qb + 1))
            if qb + 2 < NT:
                load_a(qb + 2)
                lists.append(thread_a1(qb + 2))
            for f in merge3(lists):
                f()
        sc.barrier()


def phase_f(c):
    nc, sc, S, NT = c.nc, c.sc, c.S, c.NT
    Q = min(1024, S)
    NQ = S // Q
    GQ = Q // 512
    TQ = Q // 128
    with ExitStack() as es:
        sb = lambda n, s_, d: c.sb("pf_" + n, s_, d, es)
        ps = lambda n, s_, d=F32: es.enter_context(nc.psum_tensor("pf_" + n, s_, d))
        wup = c.ffn_w_up.rearrange("(kc p) n -> p kc n", p=128)
        wd = sb("wd", [128, 22, D], BF16)
        r_wd = Reg()
        for j0 in range(0, 22, 6):
            j1 = min(22, j0 + 6)
            sc.dma("pool", wd[:, j0:j1, :], c.ffn_w_down[j0 * 128:j1 * 128, :].rearrange("(j p) d -> p j d", p=128),
                   writes=[r_wd])
        cwf = sb("cwf", [128, 44, 3], F32)
        cbf = sb("cbf", [128, 44], F32)
        gf = sb("gf", [128, D], F32)
        g2 = sb("g2", [128, D], F32)
        r_cst = Reg()
        sc.dma("sp", cwf[:], c.ffn_conv_w, writes=[r_cst])
        sc.dma("sp", cbf[:], c.ffn_conv_b, writes=[r_cst])
        sc.dma("sp", gf[:], c.final_g.partition_broadcast(128), writes=[r_cst])
        sc.dma("sp", g2[:], c.norm2_g.partition_broadcast(128), writes=[r_cst])
        halo = sb("halo", [128, 44, 2], F32)
        r_halo = regs(44)
        sc.op("dve", lambda e: e.memset(halo[:], 0.0), writes=r_halo)
        NW = 6
        wbf = [sb("w%d" % i, [128, 8, 128], BF16) for i in range(NW)]
        r_w = regs(NW)
        pm = [ps("pm%d" % i, [128, 512]) for i in range(3)]
        r_pm = regs(3)
        po = [ps("po%d" % i, [128, 512]) for i in range(3)]
        r_po = regs(3)
        pT = [ps("pT%d" % i, [128, 8, 128], BF16) for i in range(2)]
        r_pT = regs(2)
        pre = [sb("pre%d" % i, [128, 2 + Q], F32) for i in range(4)]
        r_pre = regs(4)
        cv = [sb("cv%d" % i, [128, Q], F32) for i in range(4)]
        r_cv = regs(4)
        actq = sb("actq", [128, 22, Q], BF16)
        r_actq = regs(22)
        hnT = [sb("hnT%d" % i, [128, 8, Q], BF16) for i in range(2)]
        r_hnT = [regs(TQ), regs(TQ)]
        ht = [sb("ht%d" % i, [128, D], F32) for i in range(2)]
        r_ht = regs(2)
        xn = [sb("xn%d" % i, [128, D], F32) for i in range(2)]
        r_xn = regs(2)
        ub = [sb("ub%d" % i, [128, D], BF16) for i in range(2)]
        r_ub = regs(2)
        junk = sb("junk", [128, D], BF16)
        r_junk = Reg()
        ss = [sb("ss%d" % i, [128, 2], F32) for i in range(2)]
        r_ss = regs(2)
        nss = [sb("nss%d" % i, [128, 1], F32) for i in range(2)]
        nrs = [sb("nrs%d" % i, [128, 1], F32) for i in range(2)]
        r_nss, r_nrs = regs(2), regs(2)
        cnt = {"w": 0, "pm": 0, "po": 0, "n": 0}

        def norm_quarter(q):
            hb = q % 2

            def part(tt, pt):
                t = q * TQ + tt
                b = tt % 2
                if pt == 1:
                    sc.dma("sp", xn[b][:], c.s_h[t * 128:(t + 1) * 128, :], reads=[c.r_h[t]], writes=[r_xn[b]])
                norm_transpose_tile(c, t, xn[b], r_xn[b], g2, r_cst, junk, r_junk, ub[b], r_ub[b], nss[b], r_nss[b],
                                    nrs[b], r_nrs[b], pT[b], r_pT[b],
                                    dst=hnT[hb][:, :, tt * 128:(tt + 1) * 128], r_dst=r_hnT[hb][tt], part=pt)
            part(0, 1)
            for tt in range(TQ):
                if tt + 1 < TQ:
                    part(tt + 1, 1)
                part(tt, 2)

        norm_quarter(0)
        for q in range(NQ):
            hb = q % 2
            for j in range(22):
                for gv in range(2):
                    bl = gv * 22 + j
                    pb = (j % 2) * 2 + gv
                    wi = cnt["w"] % NW
                    cnt["w"] += 1
                    sc.dma("pool", wbf[wi][:], wup[:, :, bl * 128:(bl + 1) * 128], writes=[r_w[wi]])
                    sc.op("act", lambda e: e.copy(out=pre[pb][:, 0:2], in_=halo[:, bl, :]),
                          reads=[r_halo[bl]], writes=[r_pre[pb]])
                    for g in range(GQ):
                        pi = cnt["pm"] % 3
                        cnt["pm"] += 1

                        def mm(e):
                            for kc in range(8):
                                ins = e.matmul(pm[pi][:], lhsT=wbf[wi][:, kc, :], rhs=hnT[hb][:, kc, g * 512:(g + 1) * 512],
                                               start=(kc == 0), stop=(kc == 7))
                            return ins
                        sc.op("pe", mm, reads=[r_w[wi]] + r_hnT[hb][g * 4:g * 4 + 4], writes=[r_pm[pi]])
                        sc.op("act", lambda e: e.copy(out=pre[pb][:, 2 + g * 512:2 + (g + 1) * 512], in_=pm[pi][:]),
                              reads=[], writes=[r_pre[pb]], excl=[r_pm[pi]])
                    sc.op("act", lambda e: e.copy(out=halo[:, bl, :], in_=pre[pb][:, Q:Q + 2]),
                          reads=[r_pre[pb]], writes=[r_halo[bl]])
                    a = cv[pb]
                    sc.op("act", lambda e: e.activation(out=a[:], in_=pre[pb][:, 0:Q], func=AF.Identity,
                                                        scale=cwf[:, bl, 0:1], bias=cbf[:, bl:bl + 1]),
                          reads=[r_pre[pb], r_cst], writes=[r_cv[pb]])
                    for k in range(1, 3):
                        sc.op("dve", lambda e: e.scalar_tensor_tensor(out=a[:], in0=pre[pb][:, k:k + Q],
                                                                      scalar=cwf[:, bl, k:k + 1], in1=a[:],
                                                                      op0=ALU.mult, op1=ALU.add),
                              reads=[r_pre[pb], r_cst], writes=[r_cv[pb]])
                pg, pv = (j % 2) * 2, (j % 2) * 2 + 1
                sc.op("act", lambda e: e.activation(out=cv[pg][:], in_=cv[pg][:], func=AF.Silu),
                      reads=[], writes=[r_cv[pg]])
                sc.op("dve", lambda e: e.tensor_tensor(out=actq[:, j, :], in0=cv[pg][:], in1=cv[pv][:], op=ALU.mult),
                      reads=[r_cv[pg], r_cv[pv]], writes=[r_actq[j]])
            if q + 1 < NQ:
                norm_quarter(q + 1)
            for tt in range(TQ):
                t = q * TQ + tt
                b = t % 2
                sc.dma("sp", ht[b][:], c.s_h[t * 128:(t + 1) * 128, :], reads=[c.r_h[t]], writes=[r_ht[b]])
                for dg in range(2):
                    pi = cnt["po"] % 3
                    cnt["po"] += 1

                    def mm2(e):
                        for j in range(22):
                            ins = e.matmul(po[pi][:], lhsT=actq[:, j, tt * 128:(tt + 1) * 128],
                                           rhs=wd[:, j, dg * 512:(dg + 1) * 512], start=(j == 0), stop=(j == 21))
                        return ins
                    sc.op("pe", mm2, reads=r_actq + [r_wd], writes=[r_po[pi]])
                    sc.op("dve", lambda e: e.tensor_tensor(out=ht[b][:, dg * 512:(dg + 1) * 512], in0=po[pi][:],
                                                           in1=ht[b][:, dg * 512:(dg + 1) * 512], op=ALU.add),
                          reads=[], writes=[r_ht[b]], excl=[r_po[pi]])
                sc.op("act", lambda e: e.activation(out=junk[:], in_=ht[b][:], func=AF.Square, accum_out=ss[b][:, 0:1]),
                      reads=[r_ht[b]], writes=[r_junk, r_ss[b]])
                sc.op("act", lambda e: e.activation(out=ss[b][:, 1:2], in_=ss[b][:, 0:1], func=AF.Ln, bias=EPS,
                                                    scale=1.0 / D), reads=[], writes=[r_ss[b]])
                sc.op("act", lambda e: e.activation(out=ss[b][:, 1:2], in_=ss[b][:, 1:2], func=AF.Exp, scale=-0.5),
                      reads=[], writes=[r_ss[b]])
                sc.op("dve", lambda e: e.scalar_tensor_tensor(out=ht[b][:], in0=ht[b][:], scalar=ss[b][:, 1:2],
                                                              in1=gf[:], op0=ALU.mult, op1=ALU.mult),
                      reads=[r_ss[b], r_cst], writes=[r_ht[b]])
                sc.dma("sp", c.out[t * 128:(t + 1) * 128, :], ht[b][:], reads=[r_ht[b]], writes=[c.r_out])
        sc.barrier()


def prep_inputs(inp, b, S=4096):
    f = lambda a: np.ascontiguousarray(np.asarray(a, dtype=np.float32))
    return {
        "x": f(inp["x"][b, :S]), "norm1_g": f(inp["norm1_g"]).reshape(1, D), "w_in": f(inp["w_in"][0]),
        "gdn_conv_w": f(np.asarray(inp["gdn_conv_w"])[0].reshape(4, 24, 128).transpose(2, 1, 0)),
        "gdn_a_log": f(inp["gdn_a_log"]).reshape(1, 8), "gdn_dt_bias": f(inp["gdn_dt_bias"]).reshape(1, 8),
        "gdn_norm_g": f(inp["gdn_norm_g"]).reshape(128, 1), "idx_k_norm_g": f(inp["idx_k_norm_g"]).reshape(1, 64),
        "branch_gate_b": f(np.asarray(inp["branch_gate_b"]).reshape(16, 128).T),
        "w_out": f(inp["w_out"][0]), "norm2_g": f(inp["norm2_g"]).reshape(1, D),
        "ffn_w_up": f(inp["ffn_w_up"][0]),
        "ffn_conv_w": f(np.asarray(inp["ffn_conv_w"])[0].reshape(3, 44, 128).transpose(2, 1, 0)),
        "ffn_conv_b": f(np.asarray(inp["ffn_conv_b"]).reshape(44, 128).T),
        "ffn_w_down": f(inp["ffn_w_down"][0]), "final_g": f(inp["final_g"]).reshape(1, D)}


_NC_CACHE = {}


def kernel(**inputs):
    S = 4096
    n = 8
    if S not in _NC_CACHE:
        _NC_CACHE[S] = build(S)
    nc = _NC_CACHE[S]
    in_maps = [prep_inputs(inputs, b, S) for b in range(n)]
    res = run_bass_kernel_spmd(nc, in_maps, core_ids=list(range(n)))
    out = np.stack([np.asarray(res.results[b]["out"], dtype=np.float32) for b in range(n)], axis=0)
    return out
```

```python
import numpy as np
from contextlib import ExitStack
import concourse.bass as bass
import concourse.mybir as mybir
from concourse.bass_utils import run_bass_kernel_spmd

F32 = mybir.dt.float32
BF16 = mybir.dt.bfloat16
ALU = mybir.AluOpType
AF = mybir.ActivationFunctionType
AX = mybir.AxisListType

D = 1024
NIN = 8280
DFF = 2816
EPS = 1e-6
TOPK = 256
O_QKV, O_Z, O_A, O_B, O_DQ, O_DK, O_DV, O_IQ, O_IK, O_IW, O_GA, O_GB = (
    0, 3072, 4096, 4104, 4112, 5136, 5392, 5648, 6160, 6224, 6232, 7256)


class Reg:
    __slots__ = ("name", "w", "r")

    def __init__(self, name=""):
        self.name = name
        self.w = None
        self.r = []


def regs(n, name=""):
    return [Reg("%s%d" % (name, i)) for i in range(n)]


class Sched:
    NDMA = 8

    def __init__(self, nc):
        self.nc = nc
        self.eng = {"pe": nc.tensor, "act": nc.scalar, "dve": nc.vector,
                    "pool": nc.gpsimd, "sp": nc.sync}
        self.sems = []
        self.esem = {}
        for k in self.eng:
            self.esem[k] = self._newsem("e_" + k)
        self.cnt = {k: 0 for k in self.eng}
        self.waited = {k: {} for k in self.eng}
        self.latest = {}
        self.dsem, self.duse, self.dnext = {}, {}, {}
        self.ninst = 0
        self.nwaits = 0

    def _newsem(self, name):
        self.sems.append(self.nc.alloc_semaphore(name=name))
        return len(self.sems) - 1

    def _deps(self, reads, writes, excl=(), own=None, fast=False):
        deps = {}

        def add(s, v):
            if deps.get(s, 0) < v:
                deps[s] = v
        for r in reads:
            if r.w is not None:
                add(*r.w)
        for r in excl:
            if r.w is not None:
                add(*r.w)
            for (s, v) in r.r:
                if s != own:
                    add(s, v)
        for w in writes:
            if w.w is not None:
                add(*w.w)
            for (s, v) in w.r:
                if fast and s == own:
                    continue
                add(s, v)
        return deps

    def _emit_waits(self, en, deps):
        e = self.eng[en]
        wd = self.waited[en]
        own = self.esem[en]
        for s, v in deps.items():
            if s == own and en == "pe":
                continue
            if wd.get(s, 0) >= v:
                continue
            e.wait_ge(self.sems[s], v)
            wd[s] = v
            self.nwaits += 1

    def _commit(self, ev, reads, writes):
        self.latest[ev[0]] = ev[1]
        for r in reads:
            r.r = [(s, v) for (s, v) in r.r if s != ev[0]] + [ev]
        for w in writes:
            w.w = ev
            w.r = []

    def op(self, en, fn, reads=(), writes=(), excl=()):
        own = self.esem[en]
        self._emit_waits(en, self._deps(reads, writes, excl, own, False))
        ins = fn(self.eng[en])
        ins.then_inc(self.sems[own], 1)
        self.cnt[en] += 1
        self.ninst += 1
        self._commit((own, self.cnt[en]), list(reads) + list(excl), writes)

    def dma(self, q, out, in_, reads=(), writes=(), **kw):
        if q not in self.dsem:
            self.dsem[q] = [self._newsem("d_%s%d" % (q, i)) for i in range(self.NDMA)]
            self.duse[q] = [0] * self.NDMA
            self.dnext[q] = 0
        i = self.dnext[q]
        self.dnext[q] = (i + 1) % self.NDMA
        s = self.dsem[q][i]
        deps = self._deps(reads, writes)
        if self.duse[q][i] > 0:
            v = 16 * self.duse[q][i]
            if deps.get(s, 0) < v:
                deps[s] = v
        self._emit_waits(q, deps)
        ins = self.eng[q].dma_start(out=out, in_=in_, **kw)
        ins.then_inc(self.sems[s], 16)
        self.duse[q][i] += 1
        self.ninst += 1
        self._commit((s, 16 * self.duse[q][i]), reads, writes)

    def barrier(self):
        for en in self.eng:
            self._emit_waits(en, dict(self.latest))


class Ctx:
    pass


def build(S=4096, dbg=()):
    NT = S // 128
    NG = S // 512
    nc = bass.Bass("TRN2", target_bir_lowering=False)
    c = Ctx()
    c.nc, c.S, c.NT, c.NG = nc, S, NT, NG
    for d_ in dbg:
        if d_.startswith("cstop="):
            c.cstop = int(d_[6:])
    sc = c.sc = Sched(nc)

    def din(name, shape):
        return nc.dram_tensor(name, shape, F32, kind="ExternalInput").ap()

    c.x = din("x", [S, D])
    c.norm1_g = din("norm1_g", [1, D])
    c.w_in = din("w_in", [D, NIN])
    c.gdn_conv_w = din("gdn_conv_w", [128, 24, 4])
    c.gdn_a_log = din("gdn_a_log", [1, 8])
    c.gdn_dt_bias = din("gdn_dt_bias", [1, 8])
    c.gdn_norm_g = din("gdn_norm_g", [128, 1])
    c.idx_k_norm_g = din("idx_k_norm_g", [1, 64])
    c.branch_gate_b = din("branch_gate_b", [128, 16])
    c.w_out = din("w_out", [D, D])
    c.norm2_g = din("norm2_g", [1, D])
    c.ffn_w_up = din("ffn_w_up", [D, 2 * DFF])
    c.ffn_conv_w = din("ffn_conv_w", [128, 44, 3])
    c.ffn_conv_b = din("ffn_conv_b", [128, 44])
    c.ffn_w_down = din("ffn_w_down", [DFF, D])
    c.final_g = din("final_g", [1, D])
    c.out = nc.dram_tensor("out", [S, D], F32, kind="ExternalOutput").ap()
    c.r_out = Reg("out")

    def scratch(name, shape, dt):
        kind = "ExternalOutput" if name in dbg else "Internal"
        if ("in_" + name) in dbg:
            kind = "ExternalInput"
        return nc.dram_tensor(name, shape, dt, kind=kind).ap()

    c.s_gq = scratch("s_gq", [24, 128, S], BF16)
    c.s_fa = scratch("s_fa", [8, 128, S], BF16)
    c.s_dq = scratch("s_dq", [8, 128, S], BF16)
    c.s_dk = scratch("s_dk", [2, 128, S], BF16)
    c.s_iq = scratch("s_iq", [4, 128, S], BF16)
    c.s_gb = scratch("s_gb", [8, 128, S], BF16)
    c.s_ik = scratch("s_ik", [128, S], BF16)
    c.s_v = scratch("s_v", [S, 256], BF16)
    c.s_tok = scratch("s_tok", [S, 32], F32)
    c.s_mg = scratch("s_mg", [8, 128, S], BF16)
    c.s_h = scratch("s_h", [S, D], F32)
    c.r_gq = regs(24, "gq")
    c.r_fa = regs(8, "fa")
    c.r_dq = regs(8, "dq")
    c.r_dk = regs(2, "dk")
    c.r_iq = regs(4, "iq")
    c.r_gb = regs(8, "gb")
    c.r_ik = Reg("ik")
    c.r_v = Reg("v")
    c.r_tok = Reg("tok")
    c.r_mg = regs(NT, "mg")
    c.r_h = regs(NT, "h")

    with ExitStack() as top:
        def sb(name, shape, dt, es=top):
            return es.enter_context(nc.sbuf_tensor(name, shape, dt))
        c.sb = sb
        c.identf = sb("identf", [128, 128], F32)
        c.ident = sb("ident", [128, 128], BF16)
        c.ones_bf = sb("ones_bf", [128, 128], BF16)
        c.r_const = Reg("const")
        sc.op("pool", lambda e: e.memset(c.identf[:], 0.0), writes=[c.r_const])
        sc.op("pool", lambda e: e.affine_select(
            out=c.identf[:], in_=c.identf[:], pattern=[[-1, 128]], compare_op=ALU.not_equal,
            fill=1.0, base=0, channel_multiplier=1), reads=[c.r_const], writes=[c.r_const])
        sc.op("dve", lambda e: e.tensor_copy(out=c.ident[:], in_=c.identf[:]), reads=[c.r_const], writes=[c.r_const])
        sc.op("dve", lambda e: e.memset(c.ones_bf[:], 1.0), reads=[c.r_const], writes=[c.r_const])

        if "only_f" not in dbg:
            with ExitStack() as esab:
                c.actT = sb("uT", [128, 8, S], BF16, esab)
                c.r_actT = regs(NT, "uT")
                phase_a(c, c.x, c.norm1_g, None)
                phase_b(c)
                sc.barrier()
            if "stop_b" in dbg:
                return nc
            phase_c(c)
            if "stop_c" in dbg:
                return nc
            phase_d(c)
            if "stop_d" in dbg:
                return nc
        phase_f(c)
        sc.barrier()
    return nc


def norm_transpose(c, es, tiles, g_ap, name):
    nc, sc = c.nc, c.sc
    sb = lambda n, s, d: c.sb(name + n, s, d, es)
    gb = sb("gb", [128, D], F32)
    xt = [sb("xt%d" % i, [128, D], F32) for i in range(2)]
    junk = sb("junk", [128, D], BF16)
    ub = [sb("ub%d" % i, [128, D], BF16) for i in range(2)]
    ss = [sb("ss%d" % i, [128, 1], F32) for i in range(2)]
    rstd = [sb("rstd%d" % i, [128, 1], F32) for i in range(2)]
    pT = [es.enter_context(nc.psum_tensor(name + "pT%d" % i, [128, 8, 128], BF16)) for i in range(2)]
    r_gb, r_junk = Reg(), Reg()
    r_xt, r_ub, r_ss, r_rstd, r_pT = regs(2), regs(2), regs(2), regs(2), regs(2)
    sc.dma("sp", gb[:], g_ap.partition_broadcast(128), writes=[r_gb])
    tiles = list(tiles)

    def s1(i):
        t, loader = tiles[i]
        b = i % 2
        loader(xt[b], r_xt[b])
        norm_transpose_tile(c, t, xt[b], r_xt[b], gb, r_gb, junk, r_junk, ub[b], r_ub[b], ss[b], r_ss[b],
                            rstd[b], r_rstd[b], pT[b], r_pT[b], part=1)

    def s2(i):
        t, loader = tiles[i]
        b = i % 2
        norm_transpose_tile(c, t, xt[b], r_xt[b], gb, r_gb, junk, r_junk, ub[b], r_ub[b], ss[b], r_ss[b],
                            rstd[b], r_rstd[b], pT[b], r_pT[b], part=2)
    if tiles:
        s1(0)
    for i in range(len(tiles)):
        if i + 1 < len(tiles):
            s1(i + 1)
        s2(i)


def norm_transpose_tile(c, t, xt, r_xt, gb, r_gb, junk, r_junk, ub, r_ub, ss, r_ss, rstd, r_rstd, pT, r_pT,
                        dst=None, r_dst=None, part=0):
    sc = c.sc
    if dst is None:
        dst, r_dst = c.actT[:, :, t * 128:(t + 1) * 128], c.r_actT[t]
    if part in (0, 1):
        sc.op("act", lambda e: e.activation(out=junk[:], in_=xt[:], func=AF.Square, accum_out=ss[:]),
              reads=[r_xt], writes=[r_junk, r_ss])
        sc.op("act", lambda e: e.activation(out=rstd[:], in_=ss[:], func=AF.Ln, bias=EPS, scale=1.0 / D),
              reads=[r_ss], writes=[r_rstd])
        sc.op("act", lambda e: e.activation(out=rstd[:], in_=rstd[:], func=AF.Exp, scale=-0.5),
              reads=[r_rstd], writes=[r_rstd])
    if part == 1:
        return
    sc.op("dve", lambda e: e.scalar_tensor_tensor(out=ub[:], in0=xt[:], scalar=rstd[:, 0:1], in1=gb[:],
                                                  op0=ALU.mult, op1=ALU.mult),
          reads=[r_xt, r_rstd, r_gb], writes=[r_ub])

    def tr(e):
        for kc in range(8):
            ins = e.transpose(out=pT[:, kc, :], in_=ub[:, kc * 128:(kc + 1) * 128], identity=c.ident[:])
        return ins
    sc.op("pe", tr, reads=[r_ub, c.r_const], writes=[r_pT])
    if t % 2 == 0:
        sc.op("act", lambda e: e.copy(out=dst, in_=pT[:]), reads=[], writes=[r_dst], excl=[r_pT])
    else:
        sc.op("dve", lambda e: e.tensor_copy(out=dst, in_=pT[:]), reads=[], writes=[r_dst], excl=[r_pT])


def phase_a(c, x, g, _):
    sc = c.sc
    with ExitStack() as es:
        def mk(t):
            def loader(dst, reg):
                sc.dma("sp", dst[:], x[t * 128:(t + 1) * 128, :], writes=[reg])
            return loader
        norm_transpose(c, es, [(t, mk(t)) for t in range(c.NT)], g, "pa_")
        sc.barrier()


def phase_b(c):
    nc, sc, S, NT, NG = c.nc, c.sc, c.S, c.NT, c.NG
    wv = c.w_in.rearrange("(kc p) n -> p kc n", p=128)
    with ExitStack() as es:
        sb = lambda n, s, d: c.sb("pb_" + n, s, d, es)
        ps = lambda n, s, d=F32: es.enter_context(nc.psum_tensor("pb_" + n, s, d))
        NW = 8
        wbf = [sb("w%d" % i, [128, 8, 128], BF16) for i in range(NW)]
        r_w = regs(NW)
        pm = [ps("pm%d" % i, [128, 512]) for i in range(4)]
        r_pm = regs(4)
        acc = [sb("acc%d" % i, [128, S], F32) for i in range(2)]
        r_acc = regs(2)
        yb = [sb("yb%d" % i, [128, S], BF16) for i in range(2)]
        r_yb = regs(2)
        sq = sb("sq", [128, S], BF16)
        r_sq = Reg()
        rn = sb("rn", [128, S], F32)
        r_rn = Reg()
        cw = sb("cw", [128, 24, 4], F32)
        gbias = sb("gbias", [128, 16], F32)
        gng = sb("gng", [128, 1], F32)
        r_cst = Reg()
        sc.dma("sp", cw[:], c.gdn_conv_w, writes=[r_cst])
        sc.dma("sp", gbias[:], c.branch_gate_b, writes=[r_cst])
        sc.dma("sp", gng[:], c.gdn_norm_g, writes=[r_cst])

        state = {"w": 0, "pm": 0, "row": 0}

        def row_block(col0, ncols, dst, r_dst, part0=0, foff=0):
            wi = state["w"] % NW
            state["w"] += 1
            sc.dma("pool", wbf[wi][:, :, 0:ncols], wv[:, :, col0:col0 + ncols], writes=[r_w[wi]])
            for g in range(NG):
                pi = state["pm"] % 4
                state["pm"] += 1

                def mm(e):
                    for kc in range(8):
                        ins = e.matmul(pm[pi][0:ncols, :], lhsT=wbf[wi][:, kc, 0:ncols],
                                       rhs=c.actT[:, kc, g * 512:(g + 1) * 512],
                                       start=(kc == 0), stop=(kc == 7))
                    return ins
                sc.op("pe", mm, reads=[r_w[wi]] + c.r_actT[g * 4:(g + 1) * 4], writes=[r_pm[pi]])
                en = "act" if (g % 2 == 0) else "dve"
                if en == "act":
                    sc.op("act", lambda e: e.copy(out=dst[part0:part0 + ncols, foff + g * 512:foff + (g + 1) * 512],
                                                  in_=pm[pi][0:ncols, :]),
                          reads=[r_pm[pi]], writes=[r_dst])
                else:
                    sc.op("dve", lambda e: e.tensor_copy(out=dst[part0:part0 + ncols, foff + g * 512:foff + (g + 1) * 512],
                                                         in_=pm[pi][0:ncols, :]),
                          reads=[r_pm[pi]], writes=[r_dst])

        def raw_rows(col0, nblk, s_dst, r_dst):
            for bi in range(nblk):
                b = state["row"] % 2
                state["row"] += 1
                row_block(col0 + bi * 128, 128, acc[b], r_acc[b])
                if bi % 2 == 0:
                    sc.op("act", lambda e: e.copy(out=yb[b][:], in_=acc[b][:]), reads=[r_acc[b]], writes=[r_yb[b]])
                else:
                    sc.op("dve", lambda e: e.tensor_copy(out=yb[b][:], in_=acc[b][:]), reads=[r_acc[b]], writes=[r_yb[b]])
                sc.dma("sp", s_dst[bi], yb[b][:], reads=[r_yb[b]], writes=[r_dst[bi]])
        raw_rows(O_DQ, 8, c.s_dq, c.r_dq)
        raw_rows(O_DK, 2, c.s_dk, c.r_dk)
        raw_rows(O_IQ, 4, c.s_iq, c.r_iq)

        for bi in range(8):
            b = state["row"] % 2
            state["row"] += 1
            row_block(O_GB + bi * 128, 128, acc[b], r_acc[b])
            sc.op("act", lambda e: e.activation(out=yb[b][:], in_=acc[b][:], func=AF.Sigmoid,
                                                bias=gbias[:, 8 + bi:9 + bi]),
                  reads=[r_acc[b], r_cst], writes=[r_yb[b]])
            sc.dma("sp", c.s_gb[bi], yb[b][:], reads=[r_yb[b]], writes=[c.r_gb[bi]])

        for bi in range(8):
            row_block(O_Z + bi * 128, 128, acc[0], r_acc[0])
            row_block(O_GA + bi * 128, 128, acc[1], r_acc[1])
            sc.op("act", lambda e: e.activation(out=rn[:], in_=acc[1][:], func=AF.Sigmoid,
                                                bias=gbias[:, bi:bi + 1]),
                  reads=[r_acc[1], r_cst], writes=[r_rn])
            sc.op("act", lambda e: e.activation(out=acc[1][:], in_=acc[0][:], func=AF.Silu),
                  reads=[r_acc[0]], writes=[r_acc[1]])
            sc.op("dve", lambda e: e.scalar_tensor_tensor(out=yb[0][:], in0=acc[1][:], scalar=gng[:, 0:1],
                                                          in1=rn[:], op0=ALU.mult, op1=ALU.mult),
                  reads=[r_acc[1], r_rn, r_cst], writes=[r_yb[0]])
            sc.dma("sp", c.s_fa[bi], yb[0][:], reads=[r_yb[0]], writes=[c.r_fa[bi]])

        NTK = 344
        wtok = sb("wtok", [128, 8, NTK], BF16)
        r_wtok = Reg()
        for (o, n, src) in ((0, 256, O_DV), (256, 64, O_IK), (320, 8, O_IW), (328, 8, O_A), (336, 8, O_B)):
            sc.dma("pool", wtok[:, :, o:o + n], wv[:, :, src:src + n], writes=[r_wtok])
        alog = sb("alog", [128, 8], F32)
        dtb = sb("dtb", [128, 8], F32)
        ikg = sb("ikg", [128, 64], F32)
        sc.dma("sp", alog[:], c.gdn_a_log.partition_broadcast(128), writes=[r_cst])
        sc.dma("sp", dtb[:], c.gdn_dt_bias.partition_broadcast(128), writes=[r_cst])
        sc.dma("sp", ikg[:], c.idx_k_norm_g.partition_broadcast(128), writes=[r_cst])
        sc.op("act", lambda e: e.activation(out=alog[:], in_=alog[:], func=AF.Exp), reads=[r_cst], writes=[r_cst])
        sc.op("dve", lambda e: e.tensor_scalar(out=alog[:], in0=alog[:], scalar1=-1.0, scalar2=None, op0=ALU.mult),
              reads=[r_cst], writes=[r_cst])
        vt = [sb("vt%d" % i, [128, 256], BF16) for i in range(2)]
        tk = [sb("tk%d" % i, [128, 32], F32) for i in range(2)]
        ikx = [sb("ikx%d" % i, [128, 64], F32) for i in range(2)]
        ikb = [sb("ikb%d" % i, [128, 128], BF16) for i in range(2)]
        ikT = [sb("ikT%d" % i, [128, 128], BF16) for i in range(2)]
        st = [sb("st%d" % i, [128, 4], F32) for i in range(2)]
        tmp8 = [sb("tmp8%d" % i, [128, 16], F32) for i in range(2)]
        tp = [sb("tp%d" % i, [128, 88], F32) for i in range(2)]
        r_tp = regs(2)
        pT = ps("pT", [128, 128], BF16)
        r_pT = Reg()
        r_vt, r_tk, r_ikx, r_ikb, r_ikT, r_st, r_t8 = regs(2), regs(2), regs(2), regs(2), regs(2), regs(2), regs(2)
        tok_pending = []

        def tok_flush(keep):
            while len(tok_pending) > keep:
                t = tok_pending.pop(0)
                b = t % 2
                sc.op("pe", lambda e: e.transpose(out=pT[:], in_=ikb[b][:], identity=c.ident[:]),
                      reads=[r_ikb[b], c.r_const], writes=[r_pT])
                sc.op("act", lambda e: e.copy(out=ikT[b][:], in_=pT[:]), reads=[], writes=[r_ikT[b]], excl=[r_pT])
                sc.dma("sp", c.s_ik[:, t * 128:(t + 1) * 128], ikT[b][:], reads=[r_ikT[b]], writes=[c.r_ik])

        def tok_tile(t):
            b = t % 2
            tok_flush(1)
            pi = state["pm"] % 4
            state["pm"] += 1
            P = pm[pi]

            def mm(e):
                for kc in range(8):
                    ins = e.matmul(P[:, 0:NTK], lhsT=c.actT[:, kc, t * 128:(t + 1) * 128], rhs=wtok[:, kc, :],
                                   start=(kc == 0), stop=(kc == 7))
                return ins
            sc.op("pe", mm, reads=[c.r_actT[t], r_wtok], writes=[r_pm[pi]])
            sc.op("act", lambda e: e.copy(out=vt[b][:], in_=P[:, 0:256]), reads=[], writes=[r_vt[b]], excl=[r_pm[pi]])
            sc.dma("sp", c.s_v[t * 128:(t + 1) * 128, :], vt[b][:], reads=[r_vt[b]], writes=[c.r_v])
            X = tp[b]
            sc.op("dve", lambda e: e.tensor_copy(out=X[:], in_=P[:, 256:344]), reads=[], writes=[r_tp[b]], excl=[r_pm[pi]])
            sc.op("dve", lambda e: e.tensor_reduce(out=st[b][:, 0:1], in_=X[:, 0:64], axis=AX.X, op=ALU.add),
                  reads=[r_tp[b]], writes=[r_st[b]])
            sc.op("dve", lambda e: e.tensor_scalar(out=st[b][:, 0:1], in0=st[b][:, 0:1], scalar1=-1.0 / 64,
                                                   scalar2=None, op0=ALU.mult),
                  reads=[r_st[b]], writes=[r_st[b]])
            sc.op("dve", lambda e: e.tensor_scalar(out=ikx[b][:], in0=X[:, 0:64], scalar1=st[b][:, 0:1],
                                                   scalar2=None, op0=ALU.add),
                  reads=[r_tp[b], r_st[b]], writes=[r_ikx[b]])
            sc.op("act", lambda e: e.activation(out=ikb[b][:, 0:64], in_=ikx[b][:], func=AF.Square,
                                                accum_out=st[b][:, 1:2]),
                  reads=[r_ikx[b]], writes=[r_ikb[b], r_st[b]])
            sc.op("act", lambda e: e.activation(out=st[b][:, 2:3], in_=st[b][:, 1:2], func=AF.Ln, bias=EPS,
                                                scale=1.0 / 64),
                  reads=[r_st[b]], writes=[r_st[b]])
            sc.op("act", lambda e: e.activation(out=st[b][:, 2:3], in_=st[b][:, 2:3], func=AF.Exp, scale=-0.5),
                  reads=[r_st[b]], writes=[r_st[b]])
            for hf in range(2):
                sc.op("dve", lambda e: e.scalar_tensor_tensor(out=ikb[b][:, hf * 64:(hf + 1) * 64], in0=ikx[b][:],
                                                              scalar=st[b][:, 2:3], in1=ikg[:],
                                                              op0=ALU.mult, op1=ALU.mult),
                      reads=[r_ikx[b], r_st[b], r_cst], writes=[r_ikb[b]])
            tok_pending.append(t)
            T8 = tmp8[b]
            sc.op("dve", lambda e: e.tensor_tensor(out=T8[:, 0:8], in0=X[:, 72:80], in1=dtb[:], op=ALU.add),
                  reads=[r_tp[b], r_cst], writes=[r_t8[b]])
            sc.op("act", lambda e: e.activation(out=T8[:, 0:8], in_=T8[:, 0:8], func=AF.Exp),
                  reads=[r_t8[b]], writes=[r_t8[b]])
            sc.op("act", lambda e: e.activation(out=T8[:, 0:8], in_=T8[:, 0:8], func=AF.Ln, bias=1.0),
                  reads=[r_t8[b]], writes=[r_t8[b]])
            sc.op("dve", lambda e: e.tensor_tensor(out=tk[b][:, 0:8], in0=T8[:, 0:8], in1=alog[:], op=ALU.mult),
                  reads=[r_t8[b], r_cst], writes=[r_tk[b]])
            sc.op("act", lambda e: e.activation(out=T8[:, 8:16], in_=X[:, 80:88], func=AF.Exp, scale=-1.0),
                  reads=[r_tp[b]], writes=[r_t8[b]])
            sc.op("dve", lambda e: e.tensor_scalar(out=T8[:, 8:16], in0=T8[:, 8:16], scalar1=1.0, scalar2=None,
                                                   op0=ALU.add),
                  reads=[r_t8[b]], writes=[r_t8[b]])
            sc.op("dve", lambda e: e.reciprocal(out=tk[b][:, 8:16], in_=T8[:, 8:16]),
                  reads=[r_t8[b]], writes=[r_tk[b]])
            sc.op("act", lambda e: e.activation(out=tk[b][:, 16:24], in_=X[:, 64:72], func=AF.Abs),
                  reads=[r_tp[b]], writes=[r_tk[b]])
            sc.op("act", lambda e: e.activation(out=tk[b][:, 24:32], in_=X[:, 64:72], func=AF.Sign),
                  reads=[r_tp[b]], writes=[r_tk[b]])
            sc.dma("sp", c.s_tok[t * 128:(t + 1) * 128, :], tk[b][:], reads=[r_tk[b]], writes=[c.r_tok])
        pn2 = [ps("pn%d" % i, [128, 512]) for i in range(2)]
        r_pn2 = regs(2)
        pk = ps("pk", [128, 512])
        r_pk = Reg()
        state["pn"] = 0
        oneh = sb("oneh", [128, NG, NG], BF16)
        sel = sb("sel", [NG, NG, 128], F32)
        rk = sb("rk", [NG, 512], F32)
        r_rk = Reg()
        sc.op("dve", lambda e: e.memset(oneh[:], 0.0), writes=[r_cst])
        for g_ in range(NG):
            sc.op("dve", lambda e: e.memset(oneh[:, g_, g_:g_ + 1], 1.0), reads=[r_cst], writes=[r_cst])
        selv = sel[:].rearrange("p g c -> p (g c)")
        sc.op("pool", lambda e: e.memset(selv, 1.0), reads=[r_cst], writes=[r_cst])
        sc.op("pool", lambda e: e.affine_select(out=selv, in_=selv, pattern=[[1, NG * 128]], compare_op=ALU.is_ge,
                                                fill=0.0, base=0, channel_multiplier=-128), reads=[r_cst], writes=[r_cst])
        sc.op("pool", lambda e: e.affine_select(out=selv, in_=selv, pattern=[[-1, NG * 128]], compare_op=ALU.is_ge,
                                                fill=0.0, base=127, channel_multiplier=128), reads=[r_cst], writes=[r_cst])
        preb = [sb("preb%d" % i, [128, 3 + S], BF16) for i in range(2)]
        r_preb = [regs(NG + 1), regs(NG + 1)]
        sqb = [sq, sb("sq2", [128, S], BF16)]
        r_sqg = [regs(NG), regs(NG)]
        diag = [sb("diag%d" % i, [128, 4, 128], BF16) for i in range(2)]
        r_diag = regs(2)
        r_accg = [regs(NG), regs(NG)]
        for b in range(2):
            sc.op("dve", lambda e: e.memset(preb[b][:, 0:3], 0.0), writes=[r_preb[b][NG]])
        tok_next = {"t": 0}
        wmap = {}
        for blk in range(24):
            b = blk % 2
            for j in range(4):
                sc.op("dve", lambda e: e.tensor_scalar(out=diag[b][:, j, :], in0=c.ident[:], scalar1=cw[:, blk, j:j + 1],
                                                       scalar2=None, op0=ALU.mult),
                      reads=[c.r_const, r_cst], writes=[r_diag[b]])
            PF = 4
            if blk == 0:
                for k_ in range(min(PF, 24)):
                    wmap[k_] = state["w"] % NW
                    state["w"] += 1
                    sc.dma("pool", wbf[wmap[k_]][:], wv[:, :, O_QKV + k_ * 128:O_QKV + (k_ + 1) * 128], writes=[r_w[wmap[k_]]])
            if blk + PF < 24:
                k_ = blk + PF
                wmap[k_] = state["w"] % NW
                state["w"] += 1
                sc.dma("pool", wbf[wmap[k_]][:], wv[:, :, O_QKV + k_ * 128:O_QKV + (k_ + 1) * 128], writes=[r_w[wmap[k_]]])
            wi = wmap[blk]
            a = acc[b]
            isv = blk >= 16

            def proj(g):
                pi = state["pm"] % 4
                state["pm"] += 1

                def mm(e):
                    for kc in range(8):
                        ins = e.matmul(pm[pi][:], lhsT=wbf[wi][:, kc, :], rhs=c.actT[:, kc, g * 512:(g + 1) * 512],
                                       start=(kc == 0), stop=(kc == 7))
                    return ins
                sc.op("pe", mm, reads=[r_w[wi]] + c.r_actT[g * 4:(g + 1) * 4], writes=[r_pm[pi]])
                dst = preb[b][:, 3 + g * 512:3 + (g + 1) * 512]
                if g % 2 == 0:
                    sc.op("dve", lambda e: e.tensor_copy(out=dst, in_=pm[pi][:]), reads=[], writes=[r_preb[b][g]], excl=[r_pm[pi]])
                else:
                    sc.op("act", lambda e: e.copy(out=dst, in_=pm[pi][:]), reads=[], writes=[r_preb[b][g]], excl=[r_pm[pi]])

            def conv(g):
                pi = state["pm"] % 4
                state["pm"] += 1

                def mm(e):
                    for j in range(4):
                        ins = e.matmul(pm[pi][:], lhsT=diag[b][:, j, :], rhs=preb[b][:, j + g * 512:j + (g + 1) * 512],
                                       start=(j == 0), stop=(j == 3))
                    return ins
                sc.op("pe", mm, reads=[r_diag[b], r_preb[b][g], r_preb[b][g - 1 if g > 0 else NG]], writes=[r_pm[pi]])
                if isv:
                    sc.op("act", lambda e: e.activation(out=yb[b][:, g * 512:(g + 1) * 512], in_=pm[pi][:], func=AF.Silu),
                          reads=[], writes=[r_yb[b]], excl=[r_pm[pi]])
                else:
                    gs_ = slice(g * 512, (g + 1) * 512)
                    sc.op("act", lambda e: e.activation(out=a[:, gs_], in_=pm[pi][:], func=AF.Silu),
                          reads=[], writes=[r_accg[b][g]], excl=[r_pm[pi]])
                    sc.op("dve", lambda e: e.tensor_tensor(out=sqb[b][:, gs_], in0=a[:, gs_], in1=a[:, gs_], op=ALU.mult),
                          reads=[r_accg[b][g]], writes=[r_sqg[b][g]])
            def tail(tb, tblk, part=0):
                ta = acc[tb]
                if tblk < 16 and part in (0, 1):
                    def pack(e):
                        for g in range(NG):
                            ins = e.matmul(pk[0:NG, :], lhsT=oneh[:, g, :], rhs=sqb[tb][:, g * 512:(g + 1) * 512],
                                           start=(g == 0), stop=(g == NG - 1))
                        return ins
                    sc.op("pe", pack, reads=r_sqg[tb] + [r_cst], writes=[r_pk])
                    sc.op("act", lambda e: e.activation(out=rk[:], in_=pk[0:NG, :], func=AF.Ln, bias=EPS),
                          reads=[], writes=[r_rk], excl=[r_pk])
                    sc.op("act", lambda e: e.activation(out=rk[:], in_=rk[:], func=AF.Exp, scale=-0.5),
                          reads=[], writes=[r_rk])
                if part == 1:
                    return
                if tblk < 16:
                    for g in range(NG):
                        ni = state["pn"] % 2
                        state["pn"] += 1
                        gs = slice(g * 512, (g + 1) * 512)
                        sc.op("pe", lambda e: e.matmul(pn2[ni][:], lhsT=sel[:, g, :], rhs=rk[:], start=True, stop=True),
                              reads=[r_rk, r_cst], writes=[r_pn2[ni]])
                        sc.op("dve", lambda e: e.tensor_tensor(out=yb[tb][:, gs], in0=pn2[ni][:], in1=ta[:, gs], op=ALU.mult),
                              reads=[r_accg[tb][g]], writes=[r_yb[tb]], excl=[r_pn2[ni]])
                sc.dma("sp", c.s_gq[tblk], yb[tb][:], reads=[r_yb[tb]], writes=[c.r_gq[tblk]])
                want = ((tblk + 1) * NT) // 24
                while tok_next["t"] < want:
                    tok_tile(tok_next["t"])
                    tok_next["t"] += 1

            proj(0)
            for g in range(NG):
                if g + 1 < NG:
                    proj(g + 1)
                conv(g)
                if NG >= 3 and blk >= 1:
                    if g == 1:
                        tail(1 - b, blk - 1, part=1)
                    if g == 2:
                        tail(1 - b, blk - 1, part=2)
                elif g == min(1, NG - 1) and blk >= 1:
                    tail(1 - b, blk - 1)
            if blk == 23:
                tail(b, blk)
        while tok_next["t"] < NT:
            tok_tile(tok_next["t"])
            tok_next["t"] += 1
        tok_flush(0)
        sc.barrier()


def phase_c(c):
    nc, sc, S, NT = c.nc, c.sc, c.S, c.NT
    H = 8
    with ExitStack() as es:
        sb = lambda n, s_, d: c.sb("pc_" + n, s_, d, es)
        ps = lambda n, s_, d=F32: es.enter_context(nc.psum_tensor("pc_" + n, s_, d))
        U = sb("U", [128, 128], F32)
        Ls = sb("Ls", [128, 128], F32)
        NEGI4 = sb("NEGI4", [128, 4, 128], F32)
        NEGS4 = sb("NEGS4", [128, 4, 128], F32)
        onesf = sb("onesf", [128, 128], F32)
        r_k = Reg()
        P_ = "pool"
        sc.op(P_, lambda e: e.memset(U[:], 1.0), writes=[r_k])
        sc.op(P_, lambda e: e.affine_select(out=U[:], in_=U[:], pattern=[[1, 128]], compare_op=ALU.is_ge, fill=0.0,
                                            base=0, channel_multiplier=-1), reads=[r_k], writes=[r_k])
        sc.op(P_, lambda e: e.memset(U[0:64, 64:128], 0.0), reads=[r_k], writes=[r_k])
        sc.op(P_, lambda e: e.memset(Ls[:], 1.0), reads=[r_k], writes=[r_k])
        sc.op(P_, lambda e: e.affine_select(out=Ls[:], in_=Ls[:], pattern=[[-1, 128]], compare_op=ALU.is_gt, fill=0.0,
                                            base=0, channel_multiplier=1), reads=[r_k], writes=[r_k])
        sc.op(P_, lambda e: e.memset(Ls[64:128, 0:64], 0.0), reads=[r_k], writes=[r_k])
        sc.op(P_, lambda e: e.memset(onesf[:], 1.0), reads=[r_k], writes=[r_k])
        for (dst, src) in ((NEGI4, U), (NEGS4, Ls)):
            sc.op("dve", lambda e: e.tensor_scalar(out=dst[:], in0=src[:].unsqueeze(1).to_broadcast([128, 4, 128]),
                                                   scalar1=-1.0, scalar2=30000.0, op0=ALU.add, op1=ALU.mult),
                  reads=[r_k], writes=[r_k])
        cst = [r_k, c.r_const]

        def t3(name, dt, n=1):
            return [sb("%s%d" % (name, i), [128, H, 128], dt) for i in range(n)]
        qkv = [sb("qkv%d" % i, [128, 24, 128], BF16) for i in range(2)]
        r_qkv = regs(2)
        tk = [sb("tk%d" % i, [128, 32], F32) for i in range(3)]
        r_tk = regs(3)
        fa = t3("fa", BF16, 2)
        r_fa = regs(2)
        GM, GMU = t3("GM", F32)[0], t3("GMU", F32)[0]
        r_GM, r_GMU = Reg(), Reg()
        gam_s, gamT = t3("gams", F32)[0], t3("gamT", F32)[0]
        r_gams, r_gamT = regs(2), regs(2)
        EGb = t3("EGb", F32, 2)
        r_EGb = [regs(2), regs(2)]
        eGt = [sb("eGt%d" % i, [128, 8], F32) for i in range(2)]
        r_eGt = regs(2)
        nbeta = [sb("nbeta%d" % i, [128, 8], F32) for i in range(2)]
        r_nbeta = regs(2)
        eGL = sb("eGL", [128, 8], F32)
        r_eGL = Reg()
        F32R = mybir.dt.float32r if getattr(c, "fp32r", True) else F32
        PA, PB = t3("PA", F32R, 2), t3("PB", F32R, 2)
        r_PA, r_PB = [regs(2), regs(2)], [regs(2), regs(2)]
        Y = t3("Y", F32R, 2)
        r_Y = [regs(2), regs(2)]
        attT, TTb, kd = t3("attT", BF16, 2), t3("TTb", BF16, 2), t3("kd", BF16, 2)
        r_attT, r_TTb, r_kd = [regs(2), regs(2)], [regs(H), regs(H)], [regs(H), regs(H)]
        vtok = t3("vtok", F32, 2)
        r_vtok = regs(2)
        qg = [t3("qglo", BF16, 2), t3("qghi", BF16, 2)]
        kz = [t3("kzlo", BF16, 2), t3("kzhi", BF16, 2)]
        r_qg, r_kz = [regs(2), regs(2)], [regs(2), regs(2)]
        for lst in (qg, kz):
            for a_ in lst:
                for b_ in a_:
                    sc.op(P_, lambda e: e.memset(b_[:], 0.0), writes=[r_k])
        St = t3("St", F32)[0]
        Sb = t3("Sb", BF16)[0]
        r_St, r_Sb = regs(H), regs(H)
        sc.op(P_, lambda e: e.memset(St[:], 0.0), writes=r_St)
        sc.op(P_, lambda e: e.memset(Sb[:], 0.0), writes=r_Sb)
        nR = t3("nR", BF16)[0]
        vnew = t3("vnew", BF16)[0]
        r_nR, r_vnew = regs(H), regs(H)
        oT = t3("oT", F32)[0]
        sq = t3("sq", BF16)[0]
        rs = t3("rs", F32)[0]
        mg = t3("mg", BF16, 2)
        r_oT, r_sq, r_rs, r_mg = Reg(), Reg(), Reg(), regs(2)
        pA = ps("pA", [128, H, 128])
        pB = ps("pB", [128, H, 128])
        pS = ps("pS", [128, H, 128])
        pO = ps("pO", [128, H, 128])
        r_pA, r_pB, r_pS, r_pO = regs(2), regs(2), regs(2), regs(2)
        pA_bf = pA[:].bitcast(BF16)
        pB_bf = pB[:].bitcast(BF16)
        slot = {"n": 0}
        HS = lambda hf: slice(hf * 4, hf * 4 + 4)

        def load(T):
            p = T % 2
            tsl = slice(T * 128, (T + 1) * 128)
            for i3 in range(6):
                sc.dma("sp", qkv[p][:, i3 * 4:(i3 + 1) * 4, :], c.s_gq[i3 * 4:(i3 + 1) * 4, :, tsl].rearrange("b p t -> p b t"),
                       reads=c.r_gq[i3 * 4:(i3 + 1) * 4], writes=[r_qkv[p]])
            for i3 in range(2):
                sc.dma("sp", fa[p][:, i3 * 4:(i3 + 1) * 4, :], c.s_fa[i3 * 4:(i3 + 1) * 4, :, tsl].rearrange("b p t -> p b t"),
                       reads=c.r_fa[i3 * 4:(i3 + 1) * 4], writes=[r_fa[p]])

        def load_tk(T):
            sc.dma("sp", tk[T % 3][:], c.s_tok[T * 128:(T + 1) * 128, :], reads=[c.r_tok], writes=[r_tk[T % 3]])

        def pre_steps(T):
            p = T % 2
            tkT, r_tkT = tk[T % 3], r_tk[T % 3]
            q_ = lambda h: qkv[p][:, h, :]
            k_ = lambda h: qkv[p][:, 8 + h, :]
            v_ = lambda h: qkv[p][:, 16 + h, :]
            gdec_b = tkT[:, 0:8].unsqueeze(2).to_broadcast([128, H, 128])
            steps = []

            def s0():
                sc.op(P_, lambda e: e.tensor_tensor(out=GM[:], in0=Ls[:].unsqueeze(1).to_broadcast([128, H, 128]),
                                                    in1=gdec_b, op=ALU.mult), reads=[r_tkT] + cst, writes=[r_GM])
                sc.op(P_, lambda e: e.tensor_tensor(out=GMU[:], in0=U[:].unsqueeze(1).to_broadcast([128, H, 128]),
                                                    in1=gdec_b, op=ALU.mult), reads=[r_tkT] + cst, writes=[r_GMU])
                sc.op("dve", lambda e: e.tensor_scalar(out=nbeta[p][:], in0=tkT[:, 8:16], scalar1=-1.0, scalar2=None,
                                                       op0=ALU.mult), reads=[r_tkT], writes=[r_nbeta[p]])
            steps.append(s0)

            def s1():
                for hf in range(2):
                    def mmD(e):
                        e.matmul(pA[:, HS(hf), :], lhsT=U[:], rhs=GM[:, HS(hf), :], start=True, stop=False)
                        return e.matmul(pA[:, HS(hf), :], lhsT=c.identf[:], rhs=NEGS4[:], start=False, stop=True)
                    sc.op("pe", mmD, reads=[r_GM] + cst, writes=[r_pA[hf]])

                    def mmDT(e):
                        e.matmul(pB[:, HS(hf), :], lhsT=Ls[:], rhs=GMU[:, HS(hf), :], start=True, stop=False)
                        return e.matmul(pB[:, HS(hf), :], lhsT=c.identf[:], rhs=NEGI4[:], start=False, stop=True)
                    sc.op("pe", mmDT, reads=[r_GMU] + cst, writes=[r_pB[hf]])
                    sc.op("act", lambda e: e.activation(out=gam_s[:, HS(hf), :], in_=pA[:, HS(hf), :], func=AF.Exp),
                          reads=[], writes=[r_gams[hf]], excl=[r_pA[hf]])
                    sc.op("act", lambda e: e.activation(out=gamT[:, HS(hf), :], in_=pB[:, HS(hf), :], func=AF.Exp),
                          reads=[], writes=[r_gamT[hf]], excl=[r_pB[hf]])
            steps.append(s1)

            def s2():
                for hf in range(2):
                    sc.op("pe", lambda e: e.matmul(pA[:, HS(hf), :], lhsT=onesf[:], rhs=GMU[:, HS(hf), :],
                                                   start=True, stop=True), reads=[r_GMU] + cst, writes=[r_pA[hf]])
                    sc.op("act", lambda e: e.activation(out=EGb[p][:, HS(hf), :], in_=pA[:, HS(hf), :], func=AF.Exp),
                          reads=[], writes=[r_EGb[p][hf]], excl=[r_pA[hf]])
                sc.op("pe", lambda e: e.matmul(pB[:, 0, 0:8], lhsT=U[:], rhs=tkT[:, 0:8], start=True, stop=True),
                      reads=[r_tkT] + cst, writes=[r_pB[0]])
                sc.op("act", lambda e: e.activation(out=eGt[p][:], in_=pB[:, 0, 0:8], func=AF.Exp),
                      reads=[], writes=[r_eGt[p]], excl=[r_pB[0]])
                sc.op("dve", lambda e: e.tensor_copy(out=eGL[0:64, :], in_=gamT[0:64, :, 63]),
                      reads=r_gamT, writes=[r_eGL])
                sc.op("dve", lambda e: e.tensor_copy(out=eGL[64:128, :], in_=gamT[64:128, :, 127]),
                      reads=r_gamT, writes=[r_eGL])
            steps.append(s2)

            def s3():
                for h in range(H):
                    hf = h // 4
                    sc.op("pe", lambda e: e.matmul(pA[:, h, :], lhsT=k_(h), rhs=k_(h), start=True, stop=True),
                          reads=[r_qkv[p]], writes=[r_pA[hf]])
                    sc.op("pe", lambda e: e.matmul(pB[:, h, :], lhsT=k_(h), rhs=q_(h), start=True, stop=True),
                          reads=[r_qkv[p]], writes=[r_pB[hf]])
                for h in range(H):
                    hf = h // 4
                    sc.op("dve", lambda e: e.scalar_tensor_tensor(out=PA[0][:, h, :], in0=pA[:, h, :],
                                                                  scalar=tkT[:, 8 + h:9 + h], in1=gam_s[:, h, :],
                                                                  op0=ALU.mult, op1=ALU.mult),
                          reads=[r_tkT, r_gams[hf]], writes=[r_PA[0][hf]], excl=[r_pA[hf]])
                for hf in range(2):
                    sc.op("dve", lambda e: e.scalar_tensor_tensor(out=attT[p][:, HS(hf), :], in0=pB[:, HS(hf), :],
                                                                  scalar=128.0 ** -0.5, in1=gamT[:, HS(hf), :],
                                                                  op0=ALU.mult, op1=ALU.mult),
                          reads=[r_gamT[hf]], writes=[r_attT[p][hf]], excl=[r_pB[hf]])
            steps.append(s3)

            def s4():
                for h in range(H):
                    sc.op("pe", lambda e: e.transpose(out=pA[:, h, :], in_=PA[0][:, h, :].bitcast(F32), identity=c.identf[:]),
                          reads=[r_PA[0][h // 4]] + cst, writes=[r_pA[h // 4]])
                for hf in range(2):
                    sc.op("act", lambda e: e.copy(out=PB[0][:, HS(hf), :], in_=pA[:, HS(hf), :]),
                          reads=[], writes=[r_PB[0][hf]], excl=[r_pA[hf]])
                    sc.op("dve", lambda e: e.scalar_tensor_tensor(
                        out=Y[0][:, HS(hf), :], in0=pA[:, HS(hf), :], scalar=-1.0,
                        in1=c.identf[:].unsqueeze(1).to_broadcast([128, 4, 128]), op0=ALU.mult, op1=ALU.add),
                        reads=cst, writes=[r_Y[0][hf]], excl=[r_pA[hf]])
            steps.append(s4)

            R_ = lambda ap: ap
            F_ = lambda ap: ap.bitcast(F32)

            def level(k):
                def f():
                    a, b = (k - 1) % 2, k % 2
                    for h in range(H):
                        hf = h // 4
                        sc.op("pe", lambda e: e.matmul(pA[:, h, :], lhsT=R_(PB[a][:, h, :]), rhs=R_(PA[a][:, h, :]),
                                                       start=True, stop=True),
                              reads=[r_PA[a][hf], r_PB[a][hf]], writes=[r_pA[hf]])
                        if k < 5:
                            sc.op("pe", lambda e: e.matmul(pB[:, h, :], lhsT=R_(PA[a][:, h, :]), rhs=R_(PB[a][:, h, :]),
                                                           start=True, stop=True),
                                  reads=[r_PA[a][hf], r_PB[a][hf]], writes=[r_pB[hf]])
                    for hf in range(2):
                        sc.op("act", lambda e: e.copy(out=PA[b][:, HS(hf), :], in_=pA[:, HS(hf), :]),
                              reads=[], writes=[r_PA[b][hf]], excl=[r_pA[hf]])
                        if k < 5:
                            sc.op("dve", lambda e: e.tensor_copy(out=PB[b][:, HS(hf), :], in_=pB[:, HS(hf), :]),
                                  reads=[], writes=[r_PB[b][hf]], excl=[r_pB[hf]])
                    for h in range(H):
                        hf = h // 4
                        sc.op("pe", lambda e: e.matmul(pA[:, h, :], lhsT=R_(PA[b][:, h, :]), rhs=R_(Y[a][:, h, :]),
                                                       start=True, stop=True),
                              reads=[r_PA[b][hf], r_Y[a][hf]], writes=[r_pA[hf]])
                    for hf in range(2):
                        sc.op("dve", lambda e: e.tensor_tensor(out=Y[b][:, HS(hf), :], in0=pA[:, HS(hf), :],
                                                               in1=Y[a][:, HS(hf), :].bitcast(F32), op=ALU.add),
                              reads=[r_Y[a][hf]], writes=[r_Y[b][hf]], excl=[r_pA[hf]])
                return f
            for k in range(1, 6):
                steps.append(level(k))

            def s5():
                for h in range(H):
                    sc.op("act", lambda e: e.activation(out=TTb[p][:, h, :], in_=Y[1][:, h, :].bitcast(F32),
                                                        func=AF.Copy, scale=nbeta[p][:, h:h + 1]),
                          reads=[r_Y[1][h // 4], r_nbeta[p]], writes=[r_TTb[p][h]])
                for h in range(H):
                    sc.op("pe", lambda e: e.transpose(out=pA_bf[:, h, 0:128], in_=k_(h), identity=c.ident[:]),
                          reads=[r_qkv[p]] + cst, writes=[r_pA[h // 4]])
                    sc.op("pe", lambda e: e.transpose(out=pB_bf[:, h, 0:128], in_=v_(h), identity=c.ident[:]),
                          reads=[r_qkv[p]] + cst, writes=[r_pB[h // 4]])
                for h in range(H):
                    sc.op("dve", lambda e: e.tensor_scalar(out=kd[p][:, h, :], in0=pA_bf[:, h, 0:128],
                                                           scalar1=eGL[:, h:h + 1], scalar2=None, op0=ALU.mult),
                          reads=[r_eGL], writes=[r_kd[p][h]], excl=[r_pA[h // 4]])
                sc.op("act", lambda e: e.copy(out=vtok[p][:], in_=pB_bf[:, :, 0:128]), reads=[], writes=[r_vtok[p]], excl=r_pB)
                for lo in range(2):
                    cs = slice(lo * 64, lo * 64 + 64)
                    sc.op("dve", lambda e: e.scalar_tensor_tensor(out=qg[lo][p][:, :, cs], in0=qkv[p][:, 0:8, cs],
                                                                  scalar=128.0 ** -0.5, in1=EGb[p][:, :, cs],
                                                                  op0=ALU.mult, op1=ALU.mult),
                          reads=[r_qkv[p]] + r_EGb[p], writes=[r_qg[lo][p]])
                    sc.op(P_, lambda e: e.tensor_copy(out=kz[lo][p][:, :, cs], in_=qkv[p][:, 8:16, cs]),
                          reads=[r_qkv[p]], writes=[r_kz[lo][p]])
            steps.append(s5)
            return steps

        def scan_steps(T):
            p = T % 2
            steps = []
            for ch in range(2):
                rr = slice(ch * 64, ch * 64 + 64)

                def st_p1(ch=ch, rr=rr):
                    for hf in range(2):
                        def pe(e):
                            for h in range(hf * 4, hf * 4 + 4):
                                ins = e.matmul(pS[:, h, :], lhsT=kz[ch][p][:, h, :], rhs=Sb[:, h, :], start=True, stop=True)
                            return ins
                        sc.op("pe", pe, reads=[r_kz[ch][p]] + r_Sb[hf * 4:hf * 4 + 4], writes=[r_pS[hf]])
                    for hf in range(2):
                        for h in range(hf * 4, hf * 4 + 4):
                            sc.op("dve", lambda e: e.scalar_tensor_tensor(out=nR[rr, h, :], in0=pS[rr, h, :],
                                                                          scalar=eGt[p][rr, h:h + 1], in1=vtok[p][rr, h, :],
                                                                          op0=ALU.mult, op1=ALU.subtract),
                                  reads=[r_eGt[p], r_vtok[p]], writes=[r_nR[h]], excl=[r_pS[hf]])
                steps.append(st_p1)

                def st_vn(ch=ch, rr=rr):
                    for hf in range(2):
                        def pe(e):
                            for h in range(hf * 4, hf * 4 + 4):
                                ins = e.matmul(pS[:, h, :], lhsT=TTb[p][rr, h, :], rhs=nR[rr, h, :], start=True, stop=True)
                            return ins
                        sc.op("pe", pe, reads=r_TTb[p][hf * 4:hf * 4 + 4] + r_nR[hf * 4:hf * 4 + 4], writes=[r_pS[hf]])
                    for hf in range(2):
                        sc.op("act", lambda e: e.copy(out=vnew[rr, HS(hf), :], in_=pS[rr, HS(hf), :]),
                              reads=[], writes=r_vnew[hf * 4:hf * 4 + 4] + [], excl=[r_pS[hf]])
                steps.append(st_vn)

                def st_o(ch=ch, rr=rr):
                    for hf in range(2):
                        def pe(e):
                            for h in range(hf * 4, hf * 4 + 4):
                                first = (ch == 0 and h == hf * 4)
                                e.matmul(pO[:, h, :], lhsT=Sb[:, h, :], rhs=qg[ch][p][:, h, :], start=first, stop=False,
                                         skip_group_check=True)
                                e.matmul(pO[:, h, :], lhsT=vnew[rr, h, :], rhs=attT[p][rr, h, :], start=False,
                                         stop=(ch == 1), skip_group_check=True)
                                ins = e.matmul(pS[:, h, :], lhsT=kd[p][rr, h, :], rhs=vnew[rr, h, :], start=True, stop=True)
                            return ins
                        sc.op("pe", pe, reads=r_Sb[hf * 4:hf * 4 + 4] + [r_qg[ch][p], r_attT[p][hf]] + r_vnew[hf * 4:hf * 4 + 4]
                              + r_kd[p][hf * 4:hf * 4 + 4], writes=[r_pO[hf], r_pS[hf]])
                    col = ch * 64 + 63
                    for hf in range(2):
                        for h in range(hf * 4, hf * 4 + 4):
                            sc.op("dve", lambda e: e.scalar_tensor_tensor(out=St[:, h, :], in0=St[:, h, :],
                                                                          scalar=EGb[p][:, h, col:col + 1], in1=pS[:, h, :],
                                                                          op0=ALU.mult, op1=ALU.add),
                                  reads=[r_EGb[p][hf]], writes=[r_St[h]], excl=[r_pS[hf]])
                        sc.op("act", lambda e: e.copy(out=Sb[:, HS(hf), :], in_=St[:, HS(hf), :]),
                              reads=r_St[hf * 4:hf * 4 + 4], writes=r_Sb[hf * 4:hf * 4 + 4])
                steps.append(st_o)

            def post():
                for hf in range(2):
                    sc.op("act", lambda e: e.copy(out=oT[:, HS(hf), :], in_=pO[:, HS(hf), :]),
                          reads=[], writes=[r_oT], excl=[r_pO[hf]])
                sc.op(P_, lambda e: e.tensor_tensor(out=sq[:], in0=oT[:], in1=oT[:], op=ALU.mult),
                      reads=[r_oT], writes=[r_sq])
                for hf in range(2):
                    sc.op("pe", lambda e: e.matmul(pO[:, HS(hf), :], lhsT=c.ones_bf[:], rhs=sq[:, HS(hf), :],
                                                   start=True, stop=True),
                          reads=[r_sq] + cst, writes=[r_pO[hf]])
                    sc.op("act", lambda e: e.activation(out=rs[:, HS(hf), :], in_=pO[:, HS(hf), :], func=AF.Ln,
                                                        bias=EPS, scale=1.0 / 128),
                          reads=[], writes=[r_rs], excl=[r_pO[hf]])
                sc.op("act", lambda e: e.activation(out=rs[:], in_=rs[:], func=AF.Exp, scale=-0.5),
                      reads=[r_rs], writes=[r_rs])
                sc.op("dve", lambda e: e.tensor_tensor(out=oT[:], in0=oT[:], in1=rs[:], op=ALU.mult),
                      reads=[r_oT, r_rs], writes=[r_oT])
                sc.op("dve", lambda e: e.tensor_tensor(out=mg[p][:], in0=oT[:], in1=fa[p][:], op=ALU.mult),
                      reads=[r_oT, r_fa[p]], writes=[r_mg[p]])
                for i3 in range(2):
                    sc.dma("sp", c.s_mg[i3 * 4:(i3 + 1) * 4, :, T * 128:(T + 1) * 128].rearrange("b p t -> p b t"),
                           mg[p][:, i3 * 4:(i3 + 1) * 4, :], reads=[r_mg[p]], writes=[c.r_mg[T]])
            steps.append(post)
            return steps

        load_tk(0)
        load(0)
        P0 = pre_steps(0)
        for f in P0:
            f()
        nxt = None
        if NT > 1:
            load_tk(1)
            nxt = pre_steps(1)
            nxt[0]()
        for T in range(NT):
            a = scan_steps(T)
            b = []
            nn = None
            if T + 1 < NT:
                load(T + 1)
                b = nxt[1:]
                if T + 2 < NT:
                    load_tk(T + 2)
                    nn = pre_steps(T + 2)
            n = max(len(a), len(b))
            for i in range(n):
                if i < len(b):
                    b[i]()
                if i == 1 and nn is not None:
                    nn[0]()
                if i < len(a):
                    a[i]()
            nxt = nn
        sc.barrier()


def phase_d(c):
    nc, sc, S, NT = c.nc, c.sc, c.S, c.NT
    KTOP = min(TOPK, S // 4)
    NIT = 14
    BIG = 1.0e30
    with ExitStack() as es:
        sb = lambda n, s_, d: c.sb("pd_" + n, s_, d, es)
        ps = lambda n, s_, d=F32: es.enter_context(nc.psum_tensor("pd_" + n, s_, d))
        kT = sb("kT", [128, 2, S], BF16)
        vtok = sb("vtok", [128, NT, 256], BF16)
        ikT = sb("ikT", [128, S], BF16)
        wout = sb("wout", [128, 8, D], BF16)
        negu = sb("negu", [128, 128], F32)
        posu = sb("posu", [128, 128], F32)
        pow2 = sb("pow2", [128, NIT + 1], F32)
        r_res = Reg()
        for g in range(2):
            sc.dma("sp", kT[:, g, :], c.s_dk[g], reads=c.r_dk, writes=[r_res])
        for t0 in range(0, NT, 8):
            t1 = min(NT, t0 + 8)
            sc.dma("sp", vtok[:, t0:t1, :], c.s_v[t0 * 128:t1 * 128, :].rearrange("(n p) c -> p n c", p=128),
                   reads=[c.r_v], writes=[r_res])
        sc.dma("sp", ikT[:], c.s_ik, reads=[c.r_ik], writes=[r_res])
        sc.dma("pool", wout[:], c.w_out.rearrange("(kc p) n -> p kc n", p=128), writes=[r_res])
        sc.op("pool", lambda e: e.memset(negu[:], -BIG), writes=[r_res])
        sc.op("pool", lambda e: e.affine_select(out=negu[:], in_=negu[:], pattern=[[1, 128]], compare_op=ALU.is_gt,
                                                fill=0.0, base=0, channel_multiplier=-1), reads=[r_res], writes=[r_res])
        sc.op("dve", lambda e: e.tensor_scalar(out=posu[:], in0=negu[:], scalar1=-1.0, scalar2=None, op0=ALU.mult),
              reads=[r_res], writes=[r_res])
        for n in range(NIT + 1):
            val = 2.0 ** -(n + 1) if n < NIT else 1.25 * 2.0 ** -NIT
            sc.op("pool", lambda e: e.memset(pow2[:, n:n + 1], val), reads=[r_res], writes=[r_res])
        scoreb = [sb("score%d" % i, [128, S], F32) for i in range(2)]
        r_scoreb = regs(2)
        junk = sb("junk", [128, S], BF16)
        r_junk = Reg()
        junk2 = sb("junk2", [128, S // 2], BF16)
        r_junk2, r_nm, r_sg, r_cn, r_st, r_lo = Reg(), Reg(), Reg(), Reg(), Reg(), Reg()
        mask = sb("mask", [128, S], BF16)
        r_mask = Reg()
        mbT = [sb("mbT%d" % i, [128, NT, 128], BF16) for i in range(2)]
        r_mbT = regs(2)
        tmp = [sb("tmp%d" % i, [128, 512], F32) for i in range(2)]
        r_tmp = regs(2)
        dmin = sb("dmin", [128, 128], F32)
        r_dmin = Reg()
        bs = sb("bs", [128, 8], F32)
        r_bs = Reg()
        W = sb("W", [128, NIT + 1], F32)
        r_W = Reg()
        qT = [sb("qT%d" % i, [128, 8, 128], BF16) for i in range(2)]
        iq = [sb("iq%d" % i, [128, 4, 128], BF16) for i in range(3)]
        tok = [sb("tok%d" % i, [128, 32], F32) for i in range(3)]
        gb = [sb("gb%d" % i, [128, 8, 128], BF16) for i in range(2)]
        mgb = [sb("mgb%d" % i, [128, 8, 128], BF16) for i in range(2)]
        xt = [sb("xt%d" % i, [128, D], F32) for i in range(2)]
        r_qT, r_iq, r_tok, r_gb, r_mgb, r_xt = regs(2), regs(3), regs(3), regs(2), regs(2), regs(2)
        pT_ = [sb("pT%d" % i, [128, 2, 4, 128], BF16) for i in range(2)]
        r_pTs = regs(2)
        rden = sb("rden", [128, 4, 128], F32)
        t1_ = sb("t1", [128, 4, 128], F32)
        r_rden, r_t1 = Reg(), Reg()
        mixT = sb("mixT", [128, 8, 128], BF16)
        r_mix = regs(2)
        ht = [sb("ht%d" % i, [128, D], F32) for i in range(2)]
        r_ht = regs(2)
        psc = [ps("psc%d" % i, [128, 512]) for i in range(2)]
        plg = [ps("plg%d" % i, [128, 2, 4, 128]) for i in range(2)]
        po = [ps("po0", [128, 4, 128])] * 2
        pdn = [ps("pdn0", [128, 4, 128])] * 2
        r_psc, r_plg = regs(2), regs(2)
        r_po = [Reg()] * 2
        r_pdn = [Reg()] * 2
        psc_bf = [p_[:].bitcast(BF16) for p_ in psc]
        cnt = {"psc": 0, "plg": 0, "tmp": 0, "pT": 0}

        def load_b(qb):
            p = qb % 2
            tsl = slice(qb * 128, (qb + 1) * 128)
            for i3 in range(2):
                sc.dma("sp", qT[p][:, i3 * 4:(i3 + 1) * 4, :], c.s_dq[i3 * 4:(i3 + 1) * 4, :, tsl].rearrange("b p t -> p b t"),
                       reads=c.r_dq, writes=[r_qT[p]])
                sc.dma("sp", gb[p][:, i3 * 4:(i3 + 1) * 4, :], c.s_gb[i3 * 4:(i3 + 1) * 4, :, tsl].rearrange("b p t -> p b t"),
                       reads=c.r_gb, writes=[r_gb[p]])
                sc.dma("sp", mgb[p][:, i3 * 4:(i3 + 1) * 4, :], c.s_mg[i3 * 4:(i3 + 1) * 4, :, tsl].rearrange("b p t -> p b t"),
                       reads=[c.r_mg[qb]], writes=[r_mgb[p]])
            sc.dma("sp", xt[p][:], c.x[tsl, :], writes=[r_xt[p]])

        def load_a(qb):
            p3 = qb % 3
            tsl = slice(qb * 128, (qb + 1) * 128)
            sc.dma("sp", iq[p3][:], c.s_iq[:, :, tsl].rearrange("b p t -> p b t"), reads=c.r_iq, writes=[r_iq[p3]])
            sc.dma("sp", tok[p3][:], c.s_tok[tsl, :], reads=[c.r_tok], writes=[r_tok[p3]])

        def thread_a1(qb):
            p = qb % 3
            score, r_score = scoreb[qb % 2], r_scoreb[qb % 2]
            L = (qb + 1) * 128
            steps = []
            for c0 in range(0, L, 512):
                w = min(512, L - c0)
                for h in range(8):
                    def unit(c0=c0, w=w, h=h):
                        pi = cnt["psc"] % 2
                        cnt["psc"] += 1
                        ti = cnt["tmp"] % 2
                        cnt["tmp"] += 1
                        hs = slice((h % 2) * 64, (h % 2) * 64 + 64)

                        def pe(e):
                            for o in range(0, w, 512):
                                w2 = min(512, w - o)
                                ins = e.matmul(psc[pi][:, o:o + w2], lhsT=iq[p][hs, h // 2, :], rhs=ikT[hs, c0 + o:c0 + o + w2],
                                               start=True, stop=True)
                            return ins
                        sc.op("pe", pe, reads=[r_iq[p], r_res], writes=[r_psc[pi]])
                        sc.op("act", lambda e: e.activation(out=tmp[ti][:, 0:w], in_=psc[pi][:, 0:w], func=AF.Relu,
                                                            scale=tok[p][:, 16 + h:17 + h]),
                              reads=[r_tok[p]], writes=[r_tmp[ti]], excl=[r_psc[pi]])
                        if h == 0:
                            sc.op("dve", lambda e: e.tensor_scalar(out=score[:, c0:c0 + w], in0=tmp[ti][:, 0:w],
                                                                   scalar1=tok[p][:, 24:25], scalar2=None, op0=ALU.mult),
                                  reads=[r_tmp[ti], r_tok[p]], writes=[r_score])
                        else:
                            sc.op("dve", lambda e: e.scalar_tensor_tensor(out=score[:, c0:c0 + w], in0=tmp[ti][:, 0:w],
                                                                          scalar=tok[p][:, 24 + h:25 + h],
                                                                          in1=score[:, c0:c0 + w], op0=ALU.mult, op1=ALU.add),
                                  reads=[r_tmp[ti], r_tok[p], r_score], writes=[r_score])
                    steps.append(unit)

            return steps

        def thread_a2(qb):
            p = qb % 2
            score, r_score = scoreb[qb % 2], r_scoreb[qb % 2]
            L = (qb + 1) * 128
            steps = []

            def bis_init():
                dsl = slice(qb * 128, L)
                V = "dve"
                sc.op(V, lambda e: e.tensor_tensor(out=dmin[:], in0=score[:, dsl], in1=posu[:], op=ALU.add),
                      reads=[r_score, r_res], writes=[r_dmin])
                sc.op(V, lambda e: e.tensor_reduce(out=bs[:, 0:1], in_=dmin[:], axis=AX.X, op=ALU.min),
                      reads=[r_dmin], writes=[r_bs])
                if qb > 0:
                    sc.op(V, lambda e: e.tensor_reduce(out=bs[:, 5:6], in_=score[:, 0:qb * 128], axis=AX.X, op=ALU.min),
                          reads=[r_score], writes=[r_bs])
                    sc.op(V, lambda e: e.tensor_tensor(out=bs[:, 0:1], in0=bs[:, 0:1], in1=bs[:, 5:6], op=ALU.min),
                          reads=[r_bs], writes=[r_bs])
                sc.op(V, lambda e: e.tensor_tensor(out=score[:, dsl], in0=score[:, dsl], in1=negu[:], op=ALU.add),
                      reads=[r_score, r_res, r_dmin], writes=[r_score])
                sc.op(V, lambda e: e.tensor_reduce(out=bs[:, 1:2], in_=score[:, 0:L], axis=AX.X, op=ALU.max),
                      reads=[r_score], writes=[r_bs])
                sc.op(V, lambda e: e.scalar_tensor_tensor(out=bs[:, 1:2], in0=bs[:, 1:2], scalar=1.0, in1=bs[:, 0:1],
                                                          op0=ALU.add, op1=ALU.subtract), reads=[r_bs], writes=[r_bs])
                sc.op(V, lambda e: e.tensor_scalar(out=W[:], in0=pow2[:], scalar1=bs[:, 1:2], scalar2=None, op0=ALU.mult),
                      reads=[r_bs, r_res], writes=[r_W])
                sc.op(V, lambda e: e.tensor_tensor(out=bs[:, 2:3], in0=bs[:, 0:1], in1=W[:, 0:1], op=ALU.add),
                      reads=[r_bs, r_W], writes=[r_bs, r_lo, r_cn, r_st, r_sg])
            steps.append(bis_init)
            LA = (L // 256) * 128
            LD = L - LA
            thr_c = float(KTOP) - 0.5 * LA
            for n in range(NIT):
                def bis(n=n):
                    V = "dve"
                    if LA > 0:
                        sc.op("act", lambda e: e.activation(out=junk2[:, 0:LA], in_=score[:, LD:L], func=AF.Sign,
                                                            bias=bs[:, 2:3], scale=-1.0, accum_out=bs[:, 7:8]),
                              reads=[r_score, r_bs], writes=[r_junk2, r_sg])
                    sc.op(V, lambda e: e.tensor_scalar(out=junk[:, 0:LD], in0=score[:, 0:LD], scalar1=bs[:, 2:3], scalar2=0.0,
                                                       op0=ALU.is_ge, op1=ALU.add, accum_out=bs[:, 3:4]),
                          reads=[r_score, r_bs], writes=[r_junk, r_cn])
                    if LA > 0:
                        sc.op(V, lambda e: e.scalar_tensor_tensor(out=bs[:, 3:4], in0=bs[:, 7:8], scalar=-0.5, in1=bs[:, 3:4],
                                                                  op0=ALU.mult, op1=ALU.add),
                              reads=[r_sg, r_cn], writes=[r_cn])
                    sc.op(V, lambda e: e.tensor_scalar(out=bs[:, 4:5], in0=bs[:, 3:4], scalar1=thr_c if LA > 0 else float(KTOP),
                                                       scalar2=W[:, n:n + 1], op0=ALU.is_ge, op1=ALU.mult),
                          reads=[r_cn, r_W], writes=[r_st])
                    if n + 1 < NIT:
                        sc.op(V, lambda e: e.scalar_tensor_tensor(out=bs[:, 2:3], in0=bs[:, 4:5], scalar=bs[:, 2:3],
                                                                  in1=W[:, n + 1:n + 2], op0=ALU.add, op1=ALU.subtract),
                              reads=[r_st, r_W], writes=[r_bs])
                    else:
                        sc.op(V, lambda e: e.scalar_tensor_tensor(out=bs[:, 0:1], in0=bs[:, 4:5], scalar=bs[:, 2:3],
                                                                  in1=W[:, NIT:NIT + 1], op0=ALU.add, op1=ALU.subtract),
                              reads=[r_st, r_W, r_bs], writes=[r_lo])
                steps.append(bis)

            def mk_mask():
                sc.op("dve", lambda e: e.tensor_scalar(out=mask[:, 0:L], in0=score[:, 0:L], scalar1=bs[:, 0:1], scalar2=None,
                                                       op0=ALU.is_ge), reads=[r_score, r_bs, r_lo], writes=[r_mask])
            steps.append(mk_mask)
            for k0 in range(0, qb + 1, 8):
                def tr(k0=k0):
                    k1 = min(qb + 1, k0 + 8)
                    pi = cnt["psc"] % 2
                    cnt["psc"] += 1

                    def pe(e):
                        for kb in range(k0, k1):
                            ins = e.transpose(out=psc_bf[pi][:, (kb - k0) * 128:(kb - k0 + 1) * 128],
                                              in_=mask[:, kb * 128:(kb + 1) * 128], identity=c.ident[:])
                        return ins
                    sc.op("pe", pe, reads=[r_mask, c.r_const], writes=[r_psc[pi]])
                    sc.op("act", lambda e: e.activation(out=mbT[p][:, k0:k1, :].rearrange("p a b -> p (a b)"), in_=psc_bf[pi][:, 0:(k1 - k0) * 128],
                                                        func=AF.Identity, bias=-30000.0, scale=30000.0),
                          reads=[], writes=[r_mbT[p]], excl=[r_psc[pi]])
                steps.append(tr)
            return steps

        def thread_b(qb):
            p = qb % 2
            steps = []
            units = [(g, list(range(k0, min(k0 + 2, qb + 1)))) for g in range(2) for k0 in range(0, qb + 1, 2)]
            bufs = {}

            def head(i):
                g, kbs = units[i]
                nk = len(kbs)
                li = cnt["plg"] % 2
                cnt["plg"] += 1
                ti = cnt["pT"] % 2
                cnt["pT"] += 1
                bufs[i] = ti

                def pe(e):
                    for j, kb in enumerate(kbs):
                        e.matmul(plg[li][:, j], lhsT=kT[:, g, kb * 128:(kb + 1) * 128], rhs=qT[p][:, g * 4:g * 4 + 4, :],
                                 start=True, stop=False)
                        ins = e.matmul(plg[li][:, j], lhsT=c.ident[:],
                                       rhs=mbT[p][:, kb, :].unsqueeze(1).to_broadcast([128, 4, 128]),
                                       start=False, stop=True)
                    return ins
                sc.op("pe", pe, reads=[r_res, r_qT[p], r_mbT[p], c.r_const], writes=[r_plg[li]])
                sc.op("act", lambda e: e.activation(out=pT_[ti][:, 0:nk], in_=plg[li][:, 0:nk], func=AF.Exp,
                                                    scale=128.0 ** -0.5),
                      reads=[], writes=[r_pTs[ti]], excl=[r_plg[li]])

            def tail(i):
                g, kbs = units[i]
                ti = bufs[i]

                def pe2(e):
                    for j, kb in enumerate(kbs):
                        e.matmul(po[g][:], lhsT=vtok[:, kb, g * 128:(g + 1) * 128], rhs=pT_[ti][:, j],
                                 start=(kb == 0), stop=(kb == qb))
                        ins = e.matmul(pdn[g][:], lhsT=c.ones_bf[:], rhs=pT_[ti][:, j], start=(kb == 0), stop=(kb == qb))
                    return ins
                sc.op("pe", pe2, reads=[r_res, r_pTs[ti], c.r_const], writes=[r_po[g], r_pdn[g]])

            def fin(g):
                hs = slice(g * 4, g * 4 + 4)
                sc.op("act", lambda e: e.activation(out=rden[:], in_=pdn[g][:], func=AF.Ln), reads=[], writes=[r_rden], excl=[r_pdn[g]])
                sc.op("act", lambda e: e.activation(out=rden[:], in_=rden[:], func=AF.Exp, scale=-1.0), reads=[], writes=[r_rden])
                sc.op("dve", lambda e: e.tensor_tensor(out=t1_[:], in0=po[g][:], in1=rden[:], op=ALU.mult),
                      reads=[r_rden], writes=[r_t1], excl=[r_po[g]])
                sc.op("pool", lambda e: e.tensor_tensor(out=t1_[:], in0=t1_[:], in1=gb[p][:, hs, :], op=ALU.mult),
                      reads=[r_gb[p]], writes=[r_t1])
                sc.op("pool", lambda e: e.tensor_tensor(out=mixT[:, hs, :], in0=t1_[:], in1=mgb[p][:, hs, :], op=ALU.add),
                      reads=[r_t1, r_mgb[p]], writes=[r_mix[g]])

            nu = len(units)

            def mk(i):
                def f():
                    if i + 1 < nu:
                        head(i + 1)
                    tail(i)
                    if units[i][1][-1] == qb:
                        fin(units[i][0])
                return f
            steps.append(lambda: head(0))
            for i in range(nu):
                steps.append(mk(i))

            def wo():
                for dg in range(2):
                    li = cnt["plg"] % 2
                    cnt["plg"] += 1

                    def pe(e):
                        for ec in range(8):
                            ins = e.matmul(plg[li][:, 0].rearrange("p a b -> p (a b)"), lhsT=mixT[:, ec, :],
                                           rhs=wout[:, ec, dg * 512:(dg + 1) * 512], start=(ec == 0), stop=(ec == 7))
                        return ins
                    sc.op("pe", pe, reads=r_mix + [r_res], writes=[r_plg[li]])
                    sc.op("dve", lambda e: e.tensor_tensor(out=ht[p][:, dg * 512:(dg + 1) * 512],
                                                           in0=plg[li][:, 0].rearrange("p a b -> p (a b)"),
                                                           in1=xt[p][:, dg * 512:(dg + 1) * 512], op=ALU.add),
                          reads=[r_xt[p]], writes=[r_ht[p]], excl=[r_plg[li]])
                sc.dma("sp", c.s_h[qb * 128:(qb + 1) * 128, :], ht[p][:], reads=[r_ht[p]], writes=[c.r_h[qb]])
            steps.append(wo)
            return steps

        def merge(a, b):
            na, nb = len(a), len(b)
            out, ia, ib = [], 0, 0
            while ia < na or ib < nb:
                if ib >= nb or (ia < na and ia * nb <= ib * na):
                    out.append(a[ia]); ia += 1
                else:
                    out.append(b[ib]); ib += 1
            return out

        def merge3(lists):
            lists = [l for l in lists if l]
            out = []
            idx = [0] * len(lists)
            tot = sum(len(l) for l in lists)
            while len(out) < tot:
                best, bi = None, -1
                for i, l in enumerate(lists):
                    if idx[i] < len(l):
                        frac = (idx[i] + 0.5) / len(l)
                        if best is None or frac < best:
                            best, bi = frac, i
                out.append(lists[bi][idx[bi]])
                idx[bi] += 1
            return out

        load_a(0)
        load_b(0)
        for f in thread_a1(0):
            f()
        if NT > 1:
            load_a(1)
        for f in merge3([thread_a2(0), thread_a1(1) if NT > 1 else []]):
            f()
        for qb in range(NT):
            lists = [thread_b(qb)]
            if qb + 1 < NT:
                load_b(qb + 1)
                lists.append(thread_a2(qb + 1))
            if qb + 2 < NT:
                load_a(qb + 2)
                lists.append(thread_a1(qb + 2))
            for f in merge3(lists):
                f()
        sc.barrier()


def phase_f(c):
    nc, sc, S, NT = c.nc, c.sc, c.S, c.NT
    Q = min(1024, S)
    NQ = S // Q
    GQ = Q // 512
    TQ = Q // 128
    with ExitStack() as es:
        sb = lambda n, s_, d: c.sb("pf_" + n, s_, d, es)
        ps = lambda n, s_, d=F32: es.enter_context(nc.psum_tensor("pf_" + n, s_, d))
        wup = c.ffn_w_up.rearrange("(kc p) n -> p kc n", p=128)
        wd = sb("wd", [128, 22, D], BF16)
        r_wd = Reg()
        for j0 in range(0, 22, 6):
            j1 = min(22, j0 + 6)
            sc.dma("pool", wd[:, j0:j1, :], c.ffn_w_down[j0 * 128:j1 * 128, :].rearrange("(j p) d -> p j d", p=128),
                   writes=[r_wd])
        cwf = sb("cwf", [128, 44, 3], F32)
        cbf = sb("cbf", [128, 44], F32)
        gf = sb("gf", [128, D], F32)
        g2 = sb("g2", [128, D], F32)
        r_cst = Reg()
        sc.dma("sp", cwf[:], c.ffn_conv_w, writes=[r_cst])
        sc.dma("sp", cbf[:], c.ffn_conv_b, writes=[r_cst])
        sc.dma("sp", gf[:], c.final_g.partition_broadcast(128), writes=[r_cst])
        sc.dma("sp", g2[:], c.norm2_g.partition_broadcast(128), writes=[r_cst])
        halo = sb("halo", [128, 44, 2], F32)
        r_halo = regs(44)
        sc.op("dve", lambda e: e.memset(halo[:], 0.0), writes=r_halo)
        NW = 6
        wbf = [sb("w%d" % i, [128, 8, 128], BF16) for i in range(NW)]
        r_w = regs(NW)
        pm = [ps("pm%d" % i, [128, 512]) for i in range(3)]
        r_pm = regs(3)
        po = [ps("po%d" % i, [128, 512]) for i in range(3)]
        r_po = regs(3)
        pT = [ps("pT%d" % i, [128, 8, 128], BF16) for i in range(2)]
        r_pT = regs(2)
        pre = [sb("pre%d" % i, [128, 2 + Q], F32) for i in range(4)]
        r_pre = regs(4)
        cv = [sb("cv%d" % i, [128, Q], F32) for i in range(4)]
        r_cv = regs(4)
        actq = sb("actq", [128, 22, Q], BF16)
        r_actq = regs(22)
        hnT = [sb("hnT%d" % i, [128, 8, Q], BF16) for i in range(2)]
        r_hnT = [regs(TQ), regs(TQ)]
        ht = [sb("ht%d" % i, [128, D], F32) for i in range(2)]
        r_ht = regs(2)
        xn = [sb("xn%d" % i, [128, D], F32) for i in range(2)]
        r_xn = regs(2)
        ub = [sb("ub%d" % i, [128, D], BF16) for i in range(2)]
        r_ub = regs(2)
        junk = sb("junk", [128, D], BF16)
        r_junk = Reg()
        ss = [sb("ss%d" % i, [128, 2], F32) for i in range(2)]
        r_ss = regs(2)
        nss = [sb("nss%d" % i, [128, 1], F32) for i in range(2)]
        nrs = [sb("nrs%d" % i, [128, 1], F32) for i in range(2)]
        r_nss, r_nrs = regs(2), regs(2)
        cnt = {"w": 0, "pm": 0, "po": 0, "n": 0}

        def norm_quarter(q):
            hb = q % 2

            def part(tt, pt):
                t = q * TQ + tt
                b = tt % 2
                if pt == 1:
                    sc.dma("sp", xn[b][:], c.s_h[t * 128:(t + 1) * 128, :], reads=[c.r_h[t]], writes=[r_xn[b]])
                norm_transpose_tile(c, t, xn[b], r_xn[b], g2, r_cst, junk, r_junk, ub[b], r_ub[b], nss[b], r_nss[b],
                                    nrs[b], r_nrs[b], pT[b], r_pT[b],
                                    dst=hnT[hb][:, :, tt * 128:(tt + 1) * 128], r_dst=r_hnT[hb][tt], part=pt)
            part(0, 1)
            for tt in range(TQ):
                if tt + 1 < TQ:
                    part(tt + 1, 1)
                part(tt, 2)

        norm_quarter(0)
        for q in range(NQ):
            hb = q % 2
            for j in range(22):
                for gv in range(2):
                    bl = gv * 22 + j
                    pb = (j % 2) * 2 + gv
                    wi = cnt["w"] % NW
                    cnt["w"] += 1
                    sc.dma("pool", wbf[wi][:], wup[:, :, bl * 128:(bl + 1) * 128], writes=[r_w[wi]])
                    sc.op("act", lambda e: e.copy(out=pre[pb][:, 0:2], in_=halo[:, bl, :]),
                          reads=[r_halo[bl]], writes=[r_pre[pb]])
                    for g in range(GQ):
                        pi = cnt["pm"] % 3
                        cnt["pm"] += 1

                        def mm(e):
                            for kc in range(8):
                                ins = e.matmul(pm[pi][:], lhsT=wbf[wi][:, kc, :], rhs=hnT[hb][:, kc, g * 512:(g + 1) * 512],
                                               start=(kc == 0), stop=(kc == 7))
                            return ins
                        sc.op("pe", mm, reads=[r_w[wi]] + r_hnT[hb][g * 4:g * 4 + 4], writes=[r_pm[pi]])
                        sc.op("act", lambda e: e.copy(out=pre[pb][:, 2 + g * 512:2 + (g + 1) * 512], in_=pm[pi][:]),
                              reads=[], writes=[r_pre[pb]], excl=[r_pm[pi]])
                    sc.op("act", lambda e: e.copy(out=halo[:, bl, :], in_=pre[pb][:, Q:Q + 2]),
                          reads=[r_pre[pb]], writes=[r_halo[bl]])
                    a = cv[pb]
                    sc.op("act", lambda e: e.activation(out=a[:], in_=pre[pb][:, 0:Q], func=AF.Identity,
                                                        scale=cwf[:, bl, 0:1], bias=cbf[:, bl:bl + 1]),
                          reads=[r_pre[pb], r_cst], writes=[r_cv[pb]])
                    for k in range(1, 3):
                        sc.op("dve", lambda e: e.scalar_tensor_tensor(out=a[:], in0=pre[pb][:, k:k + Q],
                                                                      scalar=cwf[:, bl, k:k + 1], in1=a[:],
                                                                      op0=ALU.mult, op1=ALU.add),
                              reads=[r_pre[pb], r_cst], writes=[r_cv[pb]])
                pg, pv = (j % 2) * 2, (j % 2) * 2 + 1
                sc.op("act", lambda e: e.activation(out=cv[pg][:], in_=cv[pg][:], func=AF.Silu),
                      reads=[], writes=[r_cv[pg]])
                sc.op("dve", lambda e: e.tensor_tensor(out=actq[:, j, :], in0=cv[pg][:], in1=cv[pv][:], op=ALU.mult),
                      reads=[r_cv[pg], r_cv[pv]], writes=[r_actq[j]])
            if q + 1 < NQ:
                norm_quarter(q + 1)
            for tt in range(TQ):
                t = q * TQ + tt
                b = t % 2
                sc.dma("sp", ht[b][:], c.s_h[t * 128:(t + 1) * 128, :], reads=[c.r_h[t]], writes=[r_ht[b]])
                for dg in range(2):
                    pi = cnt["po"] % 3
                    cnt["po"] += 1

                    def mm2(e):
                        for j in range(22):
                            ins = e.matmul(po[pi][:], lhsT=actq[:, j, tt * 128:(tt + 1) * 128],
                                           rhs=wd[:, j, dg * 512:(dg + 1) * 512], start=(j == 0), stop=(j == 21))
                        return ins
                    sc.op("pe", mm2, reads=r_actq + [r_wd], writes=[r_po[pi]])
                    sc.op("dve", lambda e: e.tensor_tensor(out=ht[b][:, dg * 512:(dg + 1) * 512], in0=po[pi][:],
                                                           in1=ht[b][:, dg * 512:(dg + 1) * 512], op=ALU.add),
                          reads=[], writes=[r_ht[b]], excl=[r_po[pi]])
                sc.op("act", lambda e: e.activation(out=junk[:], in_=ht[b][:], func=AF.Square, accum_out=ss[b][:, 0:1]),
                      reads=[r_ht[b]], writes=[r_junk, r_ss[b]])
                sc.op("act", lambda e: e.activation(out=ss[b][:, 1:2], in_=ss[b][:, 0:1], func=AF.Ln, bias=EPS,
                                                    scale=1.0 / D), reads=[], writes=[r_ss[b]])
                sc.op("act", lambda e: e.activation(out=ss[b][:, 1:2], in_=ss[b][:, 1:2], func=AF.Exp, scale=-0.5),
                      reads=[], writes=[r_ss[b]])
                sc.op("dve", lambda e: e.scalar_tensor_tensor(out=ht[b][:], in0=ht[b][:], scalar=ss[b][:, 1:2],
                                                              in1=gf[:], op0=ALU.mult, op1=ALU.mult),
                      reads=[r_ss[b], r_cst], writes=[r_ht[b]])
                sc.dma("sp", c.out[t * 128:(t + 1) * 128, :], ht[b][:], reads=[r_ht[b]], writes=[c.r_out])
        sc.barrier()


def prep_inputs(inp, b, S=4096):
    f = lambda a: np.ascontiguousarray(np.asarray(a, dtype=np.float32))
    return {
        "x": f(inp["x"][b, :S]), "norm1_g": f(inp["norm1_g"]).reshape(1, D), "w_in": f(inp["w_in"][0]),
        "gdn_conv_w": f(np.asarray(inp["gdn_conv_w"])[0].reshape(4, 24, 128).transpose(2, 1, 0)),
        "gdn_a_log": f(inp["gdn_a_log"]).reshape(1, 8), "gdn_dt_bias": f(inp["gdn_dt_bias"]).reshape(1, 8),
        "gdn_norm_g": f(inp["gdn_norm_g"]).reshape(128, 1), "idx_k_norm_g": f(inp["idx_k_norm_g"]).reshape(1, 64),
        "branch_gate_b": f(np.asarray(inp["branch_gate_b"]).reshape(16, 128).T),
        "w_out": f(inp["w_out"][0]), "norm2_g": f(inp["norm2_g"]).reshape(1, D),
        "ffn_w_up": f(inp["ffn_w_up"][0]),
        "ffn_conv_w": f(np.asarray(inp["ffn_conv_w"])[0].reshape(3, 44, 128).transpose(2, 1, 0)),
        "ffn_conv_b": f(np.asarray(inp["ffn_conv_b"]).reshape(44, 128).T),
        "ffn_w_down": f(inp["ffn_w_down"][0]), "final_g": f(inp["final_g"]).reshape(1, D)}


_NC_CACHE = {}


def kernel(**inputs):
    S = 4096
    n = 8
    if S not in _NC_CACHE:
        _NC_CACHE[S] = build(S)
    nc = _NC_CACHE[S]
    in_maps = [prep_inputs(inputs, b, S) for b in range(n)]
    res = run_bass_kernel_spmd(nc, in_maps, core_ids=list(range(n)))
    out = np.stack([np.asarray(res.results[b]["out"], dtype=np.float32) for b in range(n)], axis=0)
    return out
```
